# Optimizing a Trainium2 kernel written in Bass

```python
import math
import jax, jax.numpy as jnp
from jax import lax
import numpy as np

D_MODEL = 1024
BATCH = 4
SEQ = 4096
DEPTH = 1

HEAD_DIM = 64
N_Q_HEADS = 8
N_KV_HEADS = 2
GROUP = N_Q_HEADS // N_KV_HEADS
ATT_WIDTH = N_Q_HEADS * HEAD_DIM
KV_WIDTH = N_KV_HEADS * HEAD_DIM
WINDOW = 128
BLOCK = 128
CONV_WIDTH = D_MODEL // 2
CONV_K = 3
D_FF = 2816
N_BUCKETS = 32
MAX_DISTANCE = 128
LN_EPS = 1e-5
DEEPNORM_ALPHA = (2 * DEPTH) ** 0.25
DEEPNORM_BETA = (8 * DEPTH) ** -0.25
IN_PROJ_WIDTH = ATT_WIDTH + 2 * KV_WIDTH + 3 * CONV_WIDTH + 2 * D_MODEL
MASK_VALUE = -1e30

kernel_name = "hybrid_swa_shortconv_convffn_deepnorm"


def layer_norm(x, g, b):
    xf = x.astype(jnp.float32)
    mu = jnp.mean(xf, axis=-1, keepdims=True)
    var = jnp.mean(jnp.square(xf - mu), axis=-1, keepdims=True)
    return ((xf - mu) * lax.rsqrt(var + LN_EPS)).astype(x.dtype) * g + b


def dwconv3(x, w, b):
    xp = jnp.pad(x, ((0, 0), (1, 1), (0, 0)))
    return xp[:, :-2] * w[0] + xp[:, 1:-1] * w[1] + xp[:, 2:] * w[2] + b


def relative_bucket(rel):
    half = N_BUCKETS // 2
    max_exact = half // 2
    offset = jnp.where(rel > 0, half, 0)
    n = jnp.abs(rel)
    nf = jnp.maximum(n, 1).astype(jnp.float32)
    large = max_exact + (jnp.log(nf / max_exact) / math.log(MAX_DISTANCE / max_exact)
                         * (half - max_exact)).astype(jnp.int32)
    large = jnp.minimum(large, half - 1)
    return offset + jnp.where(n < max_exact, n, large)


def windowed_gqa(q, k, v, sink, rel_bias):
    bsz, seq = q.shape[0], q.shape[1]
    nb = seq // BLOCK
    qb = q.reshape(bsz, nb, BLOCK, N_KV_HEADS, GROUP, HEAD_DIM)

    def band(t):
        t = t.reshape(bsz, seq, N_KV_HEADS, HEAD_DIM)
        tp = jnp.pad(t, ((0, 0), (BLOCK, BLOCK), (0, 0), (0, 0)))
        tb = tp.reshape(bsz, nb + 2, BLOCK, N_KV_HEADS, HEAD_DIM)
        return jnp.concatenate([tb[:, :-2], tb[:, 1:-1], tb[:, 2:]], axis=2)

    kb, vb = band(k), band(v)
    scale = HEAD_DIM ** -0.5
    scores = jnp.einsum('bnqkgd,bnckd->bnkgqc', qb, kb).astype(jnp.float32) * scale

    qi = jnp.arange(BLOCK)[:, None]
    kc = jnp.arange(3 * BLOCK)[None, :]
    rel = kc - BLOCK - qi
    bias = rel_bias.astype(jnp.float32)[relative_bucket(rel)]
    bias = jnp.transpose(bias, (2, 0, 1)).reshape(N_KV_HEADS, GROUP, BLOCK, 3 * BLOCK)
    kpos = jnp.arange(nb)[:, None] * BLOCK - BLOCK + jnp.arange(3 * BLOCK)[None, :]
    valid = ((jnp.abs(rel) <= WINDOW)[None]
             & ((kpos >= 0) & (kpos < seq))[:, None, :])
    logits = jnp.where(valid[None, :, None, None], scores + bias, MASK_VALUE)

    sink_l = jnp.broadcast_to(sink.astype(jnp.float32).reshape(N_KV_HEADS, GROUP, 1, 1),
                              logits.shape[:-1] + (1,))
    probs = jax.nn.softmax(jnp.concatenate([logits, sink_l], axis=-1), axis=-1)[..., :-1]
    out = jnp.einsum('bnkgqc,bnckd->bnqkgd', probs.astype(vb.dtype), vb)
    return out.reshape(bsz, seq, ATT_WIDTH)


def hybrid_layer(h, w_in, b_gates, attn_sink, rel_bias, conv_w, conv_b,
                 w_att_branch, w_conv_branch, w_o, ln_mix_g, ln_mix_b,
                 w_ffn_up, ffn_conv_w, ffn_conv_b, w_ffn_down, ln_ffn_g, ln_ffn_b):
    proj = h @ w_in
    q, k, v, cb, cc, cx, gates = jnp.split(
        proj,
        np.cumsum([ATT_WIDTH, KV_WIDTH, KV_WIDTH, CONV_WIDTH, CONV_WIDTH, CONV_WIDTH]).tolist(),
        axis=-1)
    att = windowed_gqa(q, k, v, attn_sink, rel_bias)
    conv = cb * dwconv3(cc * cx, conv_w, conv_b)
    g = jax.nn.sigmoid(gates + b_gates)
    g_att, g_conv = g[..., :D_MODEL], g[..., D_MODEL:]
    merged = g_att * (att @ w_att_branch) + g_conv * (conv @ w_conv_branch)
    h = layer_norm(DEEPNORM_ALPHA * h + merged @ w_o, ln_mix_g, ln_mix_b)
    up = dwconv3(h @ w_ffn_up, ffn_conv_w, ffn_conv_b)
    a, u = up[..., :D_FF], up[..., D_FF:]
    ffn = (jax.nn.silu(a) * u) @ w_ffn_down
    return layer_norm(DEEPNORM_ALPHA * h + ffn, ln_ffn_g, ln_ffn_b)


def setup_inputs(seed: int = 0) -> dict:
    key = jax.random.key(seed)
    ks = jax.random.split(key, 24)
    f32 = jnp.float32
    L = DEPTH
    beta = DEEPNORM_BETA

    def nrm(k, shape, scale):
        return jax.random.normal(k, shape, f32) * scale

    col_scale = jnp.concatenate([
        jnp.ones((ATT_WIDTH + KV_WIDTH,), f32), jnp.full((KV_WIDTH,), beta, f32),
        jnp.ones((2 * CONV_WIDTH,), f32), jnp.full((CONV_WIDTH,), beta, f32),
        jnp.ones((2 * D_MODEL,), f32)])
    return {
        "x": nrm(ks[0], (BATCH, SEQ, D_MODEL), 1.0),
        "ln_in_g": 1.0 + nrm(ks[1], (D_MODEL,), 0.02),
        "ln_in_b": nrm(ks[2], (D_MODEL,), 0.02),
        "w_in": nrm(ks[3], (L, D_MODEL, IN_PROJ_WIDTH), D_MODEL ** -0.5) * col_scale,
        "b_gates": nrm(ks[4], (L, 2 * D_MODEL), 0.1),
        "attn_sink": nrm(ks[5], (L, N_Q_HEADS), 0.5),
        "rel_bias": nrm(ks[6], (N_BUCKETS, N_Q_HEADS), 0.2),
        "conv_w": nrm(ks[7], (L, CONV_K, CONV_WIDTH), CONV_K ** -0.5),
        "conv_b": nrm(ks[8], (L, CONV_WIDTH), 0.02),
        "w_att_branch": nrm(ks[9], (L, ATT_WIDTH, D_MODEL), beta * ATT_WIDTH ** -0.5),
        "w_conv_branch": nrm(ks[10], (L, CONV_WIDTH, D_MODEL), beta * CONV_WIDTH ** -0.5),
        "w_o": nrm(ks[11], (L, D_MODEL, D_MODEL), beta * D_MODEL ** -0.5),
        "ln_mix_g": 1.0 + nrm(ks[12], (L, D_MODEL), 0.02),
        "ln_mix_b": nrm(ks[13], (L, D_MODEL), 0.02),
        "w_ffn_up": nrm(ks[14], (L, D_MODEL, 2 * D_FF), beta * D_MODEL ** -0.5),
        "ffn_conv_w": nrm(ks[15], (L, CONV_K, 2 * D_FF), CONV_K ** -0.5),
        "ffn_conv_b": nrm(ks[16], (L, 2 * D_FF), 0.02),
        "w_ffn_down": nrm(ks[17], (L, D_FF, D_MODEL), beta * D_FF ** -0.5),
        "ln_ffn_g": 1.0 + nrm(ks[18], (L, D_MODEL), 0.02),
        "ln_ffn_b": nrm(ks[19], (L, D_MODEL), 0.02),
    }


def reference(x, ln_in_g, ln_in_b, w_in, b_gates, attn_sink, rel_bias, conv_w, conv_b,
              w_att_branch, w_conv_branch, w_o, ln_mix_g, ln_mix_b,
              w_ffn_up, ffn_conv_w, ffn_conv_b, w_ffn_down, ln_ffn_g, ln_ffn_b):
    h = layer_norm(x, ln_in_g, ln_in_b)
    for l in range(DEPTH):
        h = hybrid_layer(h, w_in[l], b_gates[l], attn_sink[l], rel_bias, conv_w[l], conv_b[l],
                         w_att_branch[l], w_conv_branch[l], w_o[l], ln_mix_g[l], ln_mix_b[l],
                         w_ffn_up[l], ffn_conv_w[l], ffn_conv_b[l], w_ffn_down[l],
                         ln_ffn_g[l], ln_ffn_b[l])
    return h
```

```python
import math
from contextlib import ExitStack

import numpy as np
import concourse.bass as bass
import concourse.mybir as mybir
from concourse.bass_utils import run_bass_kernel_spmd

F32 = mybir.dt.float32
BF16 = mybir.dt.bfloat16
AF = mybir.ActivationFunctionType
ALU = mybir.AluOpType

D = 1024
SEQ = 4096
NCORES = 8
TOK_CORE = 2048
NCH = 2
TC = TOK_CORE // NCH
NT = TC // 128
NBE = NT + 4
TE = NBE * 128
TQ = TC + 2
TP = TC + 4
QLO = 255
XROWS = TOK_CORE + 512
NBLK_CORE = XROWS // 128
DFF = 2816
NJ = DFF // 128
ALPHA = 2.0 ** 0.25
EPS = 1e-5
NEG = -30000.0
NSLOT = 6
DEBUG = False
N_INCH = 34

ENGS = ("pe", "act", "dve", "pool", "sp")


class Lane:
    def __init__(self, sem):
        self.sem = sem
        self.count = 0


class Op:
    __slots__ = ("eng", "fn", "deps", "marked", "val", "lane", "name")

    def __init__(self, eng, fn, lane=None, name=""):
        self.eng = eng
        self.fn = fn
        self.deps = set()
        self.marked = False
        self.val = None
        self.lane = lane
        self.name = name


class Buf:
    def __init__(self, name, lo=None, hi=None):
        self.name = name
        self.last_w = None
        self.readers = []
        self.lo = lo
        self.hi = hi
        self.dead = False
        self.strict = False


class Sched:
    def __init__(self):
        self.ops = {e: [] for e in ENGS}
        self.all_bufs = []

    def buf(self, name, lo=None, hi=None):
        b = Buf(name, lo, hi)
        if lo is not None:
            for o in self.all_bufs:
                if o.lo is not None and o.lo < hi and lo < o.hi:
                    if o.last_w is not None:
                        b.readers.append(o.last_w)
                    b.readers.extend(o.readers)
                    o.dead = True
            self.all_bufs.append(b)
        return b

    def buf_group(self, names, lo, hi):
        inherited = []
        for o in self.all_bufs:
            if o.lo is not None and o.lo < hi and lo < o.hi:
                if o.last_w is not None:
                    inherited.append(o.last_w)
                inherited.extend(o.readers)
                o.dead = True
        out = []
        for n in names:
            b = Buf(n, lo, hi)
            b.readers = list(inherited)
            out.append(b)
        self.all_bufs.extend(out)
        return out

    def op(self, eng, fn, reads=(), writes=(), lane=None, name=""):
        op = Op(eng, fn, lane, name)
        is_dma = lane is not None

        def same(o):
            return (not is_dma) and (o.lane is None) and o.eng == eng

        for b in reads:
            assert not b.dead, f"read of dead buf {b.name} in {name}"
            w = b.last_w
            if w is not None and not (same(w) and eng == "pe"):
                op.deps.add(w)
        for b in writes:
            assert not b.dead, f"write of dead buf {b.name} in {name}"
            w = b.last_w
            if w is not None and (not same(w) or eng != "pe"):
                op.deps.add(w)
            for r in b.readers:
                if not same(r) or eng != "pe":
                    op.deps.add(r)
        for b in reads:
            b.readers.append(op)
        for b in writes:
            b.last_w = op
            b.readers = []
        op.deps.discard(op)
        self.ops[eng].append(op)
        return op

    def emit(self, nc, es, final_waits):
        for e in ENGS:
            for op in self.ops[e]:
                for d in op.deps:
                    d.marked = True
        for d in final_waits:
            d.marked = True
        esem = {e: es.enter_context(nc.semaphore("eng_" + e)) for e in ENGS}
        cnt = {e: 0 for e in ENGS}
        for e in ENGS:
            for op in self.ops[e]:
                if op.lane is not None:
                    op.lane.count += 16
                    op.val = (op.lane.sem, op.lane.count)
                elif op.marked:
                    cnt[e] += 1
                    op.val = (esem[e], cnt[e])
        block = es.enter_context(nc.Block())

        def run(e, eng):
            seen = {}
            for op in self.ops[e]:
                need = {}
                for d in op.deps:
                    sem, v = d.val
                    k = id(sem)
                    if seen.get(k, (None, 0))[1] >= v:
                        continue
                    if k not in need or need[k][1] < v:
                        need[k] = (sem, v)
                for k, (sem, v) in need.items():
                    eng.wait_ge(sem, v)
                    seen[k] = (sem, v)
                ins = op.fn(eng)
                if op.lane is not None:
                    ins.then_inc(op.lane.sem, 16)
                elif op.marked:
                    ins.then_inc(esem[e], 1)
            if e == "sp":
                need = {}
                for d in final_waits:
                    sem, v = d.val
                    k = id(sem)
                    if k not in need or need[k][1] < v:
                        need[k] = (sem, v)
                for k, (sem, v) in need.items():
                    eng.wait_ge(sem, v)

        block.tensor(lambda eng: run("pe", eng))
        block.scalar(lambda eng: run("act", eng))
        block.vector(lambda eng: run("dve", eng))
        block.gpsimd(lambda eng: run("pool", eng))
        block.sync(lambda eng: run("sp", eng))


def build_program():
    nc = bass.Bass("TRN2", target_bir_lowering=False)

    def din(name, shape):
        return nc.dram_tensor(name, list(shape), F32, kind="ExternalInput").ap()

    x_d = din("x", [XROWS, D])
    tokvalid_d = din("tokvalid", [128, NBLK_CORE])
    kbias_d = din("kbias", [128, NBLK_CORE])
    brel_d = din("brel", [128, 3 * 8 * 128])
    sinkx_d = din("sinkx", [128, 512])
    win_d = din("win", [N_INCH, 128, 8, 128])
    bg_d = din("bg", [128, 16])
    cw_d = din("cw", [128, 12])
    cb_d = din("cbias", [128, 4])
    wa_d = din("wa", [8, 128, 4, 128])
    wc_d = din("wc", [8, 128, 4, 128])
    wo_d = din("wo", [128, 8, 1024])
    lng_d = [din("ln_in_g", [1, D]), din("ln_mix_g", [1, D]), din("ln_ffn_g", [1, D])]
    lnb_d = [din("ln_in_b", [1, D]), din("ln_mix_b", [1, D]), din("ln_ffn_b", [1, D])]
    wup_d = din("wup", [2 * NJ, 128, 8, 128])
    fcw_d = din("fcw", [128, 2 * NJ * 3])
    fcb_d = din("fcb", [128, 2 * NJ])
    wd_d = din("wd", [128, NJ, 1024])
    out_d = nc.dram_tensor("out", [TOK_CORE, D], F32, kind="ExternalOutput").ap()

    S = Sched()
    es = ExitStack()
    dbg_list = []

    def dbg(name, ap, bufs, dt=F32):
        if not DEBUG:
            return
        shape = list(ap.shape)
        dd = nc.dram_tensor("dbg_" + name, shape, dt, kind="ExternalOutput").ap()
        ln = Lane(es.enter_context(nc.semaphore("l_dbg_" + name)))
        o = S.op("sp", lambda e: e.dma_start(out=dd, in_=ap), reads=bufs, lane=ln, name="dbg_" + name)
        dbg_list.append(o)
    with es:
        AW = 53200
        arena = es.enter_context(nc.sbuf_tensor("arena", [128, AW], F32))
        ps = es.enter_context(nc.psum_tensor("ps", [128, 4096], F32))
        es.enter_context(nc.allow_low_precision(reason="bf16 matmul operands, fp32 accumulation"))

        def new_lane(name):
            return Lane(es.enter_context(nc.semaphore(name)))

        class Alloc:
            def __init__(self, lo, hi):
                self.p = lo
                self.hi = hi

            def take(self, words):
                words = (words + 7) // 8 * 8
                lo = self.p
                self.p += words
                assert self.p <= self.hi, (self.p, self.hi)
                return lo

        def f32ap(lo, n, parts=128):
            return arena[0:parts, lo:lo + n]

        def b16ap(lo, n, parts=128):
            assert n % 2 == 0
            return arena[0:parts, lo:lo + n // 2].bitcast(BF16)

        class T:
            def __init__(self, name, lo, n, dt, parts=128):
                words = n if dt is F32 else n // 2
                self.buf = S.buf(name, lo, lo + words)
                self.ap = f32ap(lo, n, parts) if dt is F32 else b16ap(lo, n, parts)

        PA = Alloc(0, AW)
        hres = [T(f"hres{t}", PA.take(1024), 1024, F32) for t in range(NT + 2)]
        hT_lo = PA.take(8 * TE // 2)
        gtile = T("gtile", PA.take(1024), 1024, F32)
        btile = T("btile", PA.take(1024), 1024, F32)
        R_W = 7168
        R_lo = PA.take(R_W)
        cur = {"wslots": None, "wlimit": 0, "brel": None, "esink": None}
        ident = T("ident", PA.take(64), 128, BF16)
        ones = T("ones", PA.take(32), 64, BF16)
        tokvalid = T("tokvalid", PA.take(NBLK_CORE), NBLK_CORE, F32)
        kbias = T("kbias", PA.take(NBLK_CORE), NBLK_CORE, F32)
        bg = T("bg", PA.take(16), 16, F32)
        cw = T("cw", PA.take(12), 12, F32)
        cbs = T("cbs", PA.take(4), 4, F32)
        fcw = T("fcw", PA.take(6 * NJ), 6 * NJ, F32)
        fcb = T("fcb", PA.take(2 * NJ), 2 * NJ, F32)
        class Stat:
            def __init__(self, i):
                self.bn = T(f"stat_bn{i}", PA.take(16), 16, F32)
                self.mv = T(f"stat_mv{i}", PA.take(8), 8, F32)
                self.rs = T(f"stat_rs{i}", PA.take(8), 8, F32)

        stat = [Stat(i) for i in range(3)]
        stat0 = [Stat(i + 3) for i in range(3)]
        OV = PA.p
        OVW = AW - OV

        bank = [S.buf(f"bank{i}") for i in range(8)]
        bank[4].strict = True
        bank[5].strict = True

        def banks_of(c0, c1):
            return [bank[i] for i in range(c0 // 512, (c1 - 1) // 512 + 1)]

        lane_ws = [new_lane(f"l_ws{i}") for i in range(NSLOT)]
        lane_x = [new_lane(f"l_x{i}") for i in range(4)]
        lane_o = [new_lane(f"l_o{i}") for i in range(4)]
        lane_g = new_lane("l_g")
        lane_b = new_lane("l_b")
        lane_g0 = new_lane("l_g0")
        lane_b0 = new_lane("l_b0")
        lane_wo = new_lane("l_wo")
        lane_wd = new_lane("l_wd")

        def const_load(t, src, eng="sp"):
            ln = new_lane("l_c_" + t.buf.name)
            S.op(eng, lambda e, t=t, src=src: e.dma_start(out=t.ap, in_=src), writes=[t.buf], lane=ln,
                 name="const_" + t.buf.name)

        const_load(tokvalid, tokvalid_d)
        S.op("pool", lambda e: e.memset(ident.ap, 0.0), writes=[ident.buf], name="ident0")
        S.op("pool", lambda e: e.affine_select(out=ident.ap, in_=ident.ap, pattern=[[-1, 128]],
                                               compare_op=ALU.not_equal, fill=1.0, base=0, channel_multiplier=1),
             reads=[ident.buf], writes=[ident.buf], name="ident1")
        S.op("pool", lambda e: e.memset(ones.ap, 1.0), writes=[ones.buf], name="ones")

        def late_consts():
            const_load(kbias, kbias_d)
            const_load(bg, bg_d)
            const_load(cw, cw_d)
            const_load(cbs, cb_d)
            const_load(fcw, fcw_d)
            const_load(fcb, fcb_d)

        def region_b(ch):
            ab = Alloc(R_lo, R_lo + R_W)
            cur["wslots"] = [T(f"wslot{ch}_{i}", ab.take(512), 1024, BF16) for i in range(NSLOT)]
            cur["brel"] = T(f"brel{ch}", ab.take(3072), 3072, F32)
            cur["esink"] = T(f"esink{ch}", ab.take(512), 512, F32)
            cur["wlimit"] = (ch + 1) * W_PER_CHUNK
            brel, esink = cur["brel"], cur["esink"]
            const_load(brel, brel_d)
            const_load(esink, sinkx_d)
            S.op("act", lambda e: e.activation(out=esink.ap, in_=esink.ap, func=AF.Exp), reads=[esink.buf],
                 writes=[esink.buf], name="esink_exp")

        wseq = []
        for _ch in range(NCH):
            wseq += [(win_d[0], 128, 8), (win_d[1], 128, 8)] + [(win_d[2 + c], 128, 8) for c in range(4)]
            for j in range(4):
                wseq += [(win_d[6 + 3 * j + 1], 128, 8), (win_d[6 + 3 * j + 2], 128, 8), (win_d[6 + 3 * j], 128, 8)]
            for f in range(8):
                wseq += [(win_d[18 + 2 * f], 128, 8), (win_d[18 + 2 * f + 1], 128, 8), (wa_d[f], 128, 4), (wc_d[f], 128, 4)]
            wseq += [(wup_d[ci], 128, 8) for ci in range(2 * NJ)]
        ring = {"cur": 0, "issued": 0}
        W_PER_CHUNK = len(wseq) // NCH

        def load_w(src_ap=None, parts=128, nk=8):
            n = ring["cur"]
            ring["cur"] += 1
            assert wseq[n][1] == parts and wseq[n][2] == nk, (n, wseq[n][1:], parts, nk)
            wslots = cur["wslots"]
            while ring["issued"] < min(len(wseq), n + NSLOT, cur["wlimit"]):
                m = ring["issued"]
                ring["issued"] += 1
                src, mp, mk = wseq[m]
                tm = wslots[m % NSLOT]
                dst = tm.ap[0:mp, 0:mk * 128].rearrange("p (k f) -> p k f", k=mk)
                S.op("pool", lambda e, dst=dst, src=src: e.dma_start(out=dst, in_=src, max_dma_last_dim=8192),
                     writes=[tm.buf], lane=lane_ws[m % NSLOT], name="wload")
            t = wslots[n % NSLOT]
            return t, t.ap.rearrange("p (k f) -> p k f", k=8)

        rr = {"i": 0}

        def region():
            r = rr["i"] % 2
            rr["i"] += 1
            return r * 1536

        def evac_engine():
            return "act"

        def ln_load_gb(which):
            S.op("sp", lambda e: e.dma_start(out=gtile.ap, in_=lng_d[which].partition_broadcast(128)),
                 writes=[gtile.buf], lane=lane_g, name="ld_g")
            S.op("sp", lambda e: e.dma_start(out=btile.ap, in_=lnb_d[which].partition_broadcast(128)),
                 writes=[btile.buf], lane=lane_b, name="ld_b")

        class Pipe:
            def __init__(self, n_items, stages):
                self.n = n_items
                self.stages = stages
                self.nsteps = n_items + len(stages) - 1

            def step(self, s):
                for k, st in enumerate(self.stages):
                    t = s - k
                    if 0 <= t < self.n:
                        st(t)

            def run(self):
                for s in range(self.nsteps):
                    self.step(s)

        def pipeline(n_items, stages):
            Pipe(n_items, stages).run()

        def ln_stats_a(src, st, junk_ap, junk_buf):
            sa = st.bn.ap
            jb = junk_buf if isinstance(junk_buf, list) else [junk_buf]
            S.op("act", lambda e: e.activation(out=junk_ap, in_=src.ap, func=AF.Identity, accum_out=sa[:, 0:1]),
                 reads=[src.buf], writes=[st.bn.buf] + jb, name="ln_sum")
            S.op("act", lambda e: e.activation(out=junk_ap, in_=src.ap, func=AF.Square, accum_out=sa[:, 1:2]),
                 reads=[src.buf, st.bn.buf], writes=[st.bn.buf] + jb, name="ln_sumsq")

        def ln_stats(src, st, from_act=False, gt=None):
            gt = gt or gtile
            sa = st.bn.ap
            mv = st.mv.ap
            if not from_act:
                S.op("dve", lambda e: e.bn_stats(out=sa[:, 0:6], in_=src.ap[:, 0:512]), reads=[src.buf], writes=[st.bn.buf])
                S.op("dve", lambda e: e.bn_stats(out=sa[:, 6:12], in_=src.ap[:, 512:1024]), reads=[src.buf, st.bn.buf],
                     writes=[st.bn.buf])
                S.op("dve", lambda e: e.bn_aggr(out=mv[:, 0:2], in_=sa[:, 0:12]), reads=[st.bn.buf], writes=[st.mv.buf])
            else:
                S.op("dve", lambda e: e.tensor_scalar(out=mv[:, 0:1], in0=sa[:, 0:1], scalar1=1.0 / D, scalar2=None,
                                                      op0=ALU.mult), reads=[st.bn.buf], writes=[st.mv.buf])
                S.op("dve", lambda e: e.tensor_scalar(out=sa[:, 2:3], in0=mv[:, 0:1], scalar1=mv[:, 0:1], scalar2=-1.0,
                                                      op0=ALU.mult, op1=ALU.mult), reads=[st.mv.buf], writes=[st.bn.buf])
                S.op("dve", lambda e: e.tensor_scalar(out=mv[:, 1:2], in0=sa[:, 1:2], scalar1=1.0 / D,
                                                      scalar2=sa[:, 2:3], op0=ALU.mult, op1=ALU.add),
                     reads=[st.bn.buf, st.mv.buf], writes=[st.mv.buf])
            S.op("dve", lambda e: e.scalar_tensor_tensor(out=src.ap, in0=src.ap, scalar=mv[:, 0:1], in1=gt.ap,
                                                         op0=ALU.subtract, op1=ALU.mult),
                 reads=[src.buf, st.mv.buf, gt.buf], writes=[src.buf])
            S.op("act", lambda e: e.activation(out=st.rs.ap[:, 0:1], in_=mv[:, 1:2], func=AF.Sqrt, bias=EPS, scale=1.0),
                 reads=[st.mv.buf], writes=[st.rs.buf])

        def ln_norm(src, dst, st, hb=None, valid_ap=None, bt=None):
            bt = bt or btile
            rs = st.rs.ap
            S.op("dve", lambda e: e.reciprocal(out=rs[:, 1:2], in_=rs[:, 0:1]), reads=[st.rs.buf], writes=[st.rs.buf])
            S.op("dve", lambda e: e.scalar_tensor_tensor(out=dst.ap, in0=src.ap, scalar=rs[:, 1:2], in1=bt.ap,
                                                         op0=ALU.mult, op1=ALU.add),
                 reads=[src.buf, st.rs.buf, bt.buf], writes=[dst.buf])
            if hb is not None:
                S.op("act", lambda e: e.activation(out=hb.ap, in_=dst.ap, func=AF.Identity, scale=valid_ap),
                     reads=[dst.buf, tokvalid.buf], writes=[hb.buf])

        def tr_mm(hb, pb):
            psb = ps[:, pb * 512:(pb + 1) * 512].bitcast(BF16)
            for kc in range(8):
                S.op("pe", lambda e, kc=kc: e.transpose(out=psb[:, kc * 128:(kc + 1) * 128],
                                                        in_=hb.ap[:, kc * 128:(kc + 1) * 128], identity=ident.ap),
                     reads=[hb.buf, ident.buf], writes=[bank[pb]])

        def tr_evac(dstT_ap, dst_buf, col0, pb):
            psb = ps[:, pb * 512:(pb + 1) * 512].bitcast(BF16)
            d3 = dstT_ap.rearrange("p (k f) -> p k f", k=8)[:, :, col0:col0 + 128]
            S.op("act", lambda e: e.activation(out=d3, in_=psb.rearrange("p (k f) -> p k f", k=8), func=AF.Copy),
                 reads=[bank[pb]], writes=[dst_buf])

        class TiledT:
            def __init__(self, name, lo):
                self.ap = b16ap(lo, 8 * TE)
                self.bufs = S.buf_group([f"{name}_{i}" for i in range(NBE)], lo, lo + 8 * TE // 2)

            def cols(self, c0, c1):
                return [self.bufs[i] for i in range(c0 // 128, (c1 - 1) // 128 + 1)]

        def proj_fm(w3, wbuf, src3, srcbufs, c_lo, ncols, r0, nk=8, kparts=128, src_k=None):
            g0 = 0
            while g0 < ncols:
                g1 = min(g0 + 512, ncols)
                for k in range(nk):
                    S.op("pe", lambda e, k=k, g0=g0, g1=g1: e.matmul(
                        out=ps[:, r0 + g0:r0 + g1], lhsT=w3[0:kparts, k, :],
                        rhs=src3[0:kparts, k, c_lo + g0:c_lo + g1], start=(k == 0), stop=(k == nk - 1)),
                        reads=[wbuf] + (srcbufs(c_lo + g0, c_lo + g1) if callable(srcbufs) else srcbufs),
                        writes=banks_of(r0 + g0, r0 + g1), name="proj")
                g0 = g1

        def do_chunk(ch):
            blk0 = ch * NT
            row0 = ch * TC
            hT = TiledT(f"hT{ch}", hT_lo)
            hT3 = hT.ap.rearrange("p (k f) -> p k f", k=8)
            A0 = Alloc(R_lo, R_lo + R_W)
            NX = 4
            xt = [T(f"xt{ch}_{i}", A0.take(1024), 1024, F32) for i in range(NX)]
            hb0 = [T(f"hb0{ch}_{i}", A0.take(512), 1024, BF16) for i in range(2)]
            g0tile = T(f"g0tile{ch}", A0.take(1024), 1024, F32)
            b0tile = T(f"b0tile{ch}", A0.take(1024), 1024, F32)
            junk_ap = ps[:, 4 * 512:6 * 512]
            S.op("sp", lambda e: e.dma_start(out=g0tile.ap, in_=lng_d[0].partition_broadcast(128)),
                 writes=[g0tile.buf], lane=lane_g0, name="ld_g0")
            S.op("sp", lambda e: e.dma_start(out=b0tile.ap, in_=lnb_d[0].partition_broadcast(128)),
                 writes=[b0tile.buf], lane=lane_b0, name="ld_b0")

            p0_order = list(range(NBE)) if ch == 0 else [0, 1, NT + 2, NT + 3] + list(range(2, NT + 2))

            def p0_dst(eb, n):
                return hres[eb - 1] if 1 <= eb <= NT + 2 else xt[n % NX]

            def p0_load(n):
                eb = p0_order[n]
                S.op("sp", lambda e: e.dma_start(out=xt[n % NX].ap, in_=x_d[row0 + eb * 128:row0 + (eb + 1) * 128, :]),
                     writes=[xt[n % NX].buf], lane=lane_x[n % NX], name="ld_x")

            def use_act(n):
                return True if ch > 0 else (n % 2 == 1)

            def p0_act(n):
                if use_act(n):
                    ln_stats_a(xt[n % NX], stat0[n % 3], junk_ap, [bank[4], bank[5]])

            def p0_norm(n):
                eb = p0_order[n]
                ln_norm(xt[n % NX], p0_dst(eb, n), stat0[n % 3], hb=hb0[n % 2],
                        valid_ap=tokvalid.ap[:, blk0 + eb:blk0 + eb + 1], bt=b0tile)

            p0 = Pipe(NBE, [
                p0_load,
                p0_act,
                lambda n: ln_stats(xt[n % NX], stat0[n % 3], from_act=use_act(n), gt=g0tile),
                p0_norm,
                lambda n: tr_mm(hb0[n % 2], 6 + n % 2),
                lambda n: tr_evac(hT.ap, hT.bufs[p0_order[n]], p0_order[n] * 128, 6 + n % 2),
            ])
            yield p0

            region_b(ch)
            brel, esink = cur["brel"], cur["esink"]
            OA = Alloc(OV, AW)
            mT = [T(f"mT{ch}_{f}", OA.take(640), 1280, BF16) for f in range(8)]
            attT = T(f"attT{ch}", OA.take(4 * TQ // 2), 4 * TQ, BF16)
            attT3 = attT.ap.rearrange("p (g q) -> p g q", g=4)
            convT = T(f"convT{ch}", OA.take(4 * TQ // 2), 4 * TQ, BF16)
            convT3 = convT.ap.rearrange("p (j q) -> p j q", j=4)
            SA = OA.take(7184)
            SB = OA.take(4096)
            SC = OA.take(3 * 1032)
            if ch == 0:
                dbg("h_t1", hres[1].ap, [hres[1].buf])
                dbg("hT", hT.ap, hT.bufs, BF16)
            if ch == 0:
                late_consts()
            for f in range(8):
                S.op("dve", lambda e, f=f: e.memset(mT[f].ap[:, 0:QLO - 128], 0.0), writes=[mT[f].buf], name="mT0")
                S.op("dve", lambda e, f=f: e.memset(mT[f].ap[:, QLO - 128 + TQ:1280], 0.0), writes=[mT[f].buf], name="mT0")

            A1 = Alloc(SA, SA + 7184 + 4096)
            kT = T(f"kT{ch}", A1.take(TE // 2), TE, BF16)
            vt = T(f"v{ch}", A1.take(TE // 2), TE, BF16)
            vt3 = vt.ap.rearrange("p (b f) -> p b f", b=NBE)
            qT = T(f"qT{ch}", A1.take(4 * TQ // 2), 4 * TQ, BF16)
            qT3 = qT.ap.rearrange("p (c q) -> p c q", c=4)
            tt = [T(f"tt{ch}_{i}", A1.take(1024), 1024, F32) for i in range(2)]
            PTr = [T(f"PT{ch}_{i}", A1.take(512), 1024, BF16) for i in range(3)]
            dtmp = [T(f"dtmp{ch}_{i}", A1.take(512), 512, F32) for i in range(2)]

            wt, w3 = load_w(win_d[0])
            r0 = region()
            proj_fm(w3, wt.buf, hT3, hT.cols, 0, TE, r0)
            S.op("act", lambda e, r0=r0: e.activation(out=kT.ap, in_=ps[:, r0:r0 + TE], func=AF.Copy),
                 reads=banks_of(r0, r0 + TE), writes=[kT.buf], name="evac_k")
            wt, w3 = load_w(win_d[1])
            r0 = region()
            for eb in range(NBE):
                for k in range(8):
                    S.op("pe", lambda e, k=k, eb=eb, r0=r0, w3=w3: e.matmul(
                        out=ps[:, r0 + eb * 128:r0 + (eb + 1) * 128], lhsT=hT3[:, k, eb * 128:(eb + 1) * 128],
                        rhs=w3[:, k, :], start=(k == 0), stop=(k == 7)),
                        reads=[wt.buf, hT.bufs[eb]], writes=banks_of(r0 + eb * 128, r0 + (eb + 1) * 128), name="vproj")
            S.op("dve", lambda e, r0=r0: e.tensor_copy(out=vt.ap, in_=ps[:, r0:r0 + TE]),
                 reads=banks_of(r0, r0 + TE), writes=[vt.buf], name="evac_v")
            for c in range(4):
                wt, w3 = load_w(win_d[2 + c])
                r0 = region()
                proj_fm(w3, wt.buf, hT3, hT.cols, QLO, TQ, r0)
                eng = "act" if c % 2 == 0 else "dve"
                if eng == "act":
                    S.op("act", lambda e, r0=r0, c=c: e.activation(out=qT3[:, c, :], in_=ps[:, r0:r0 + TQ], func=AF.Copy),
                         reads=banks_of(r0, r0 + TQ), writes=[qT.buf], name="evac_q")
                else:
                    S.op("dve", lambda e, r0=r0, c=c: e.tensor_copy(out=qT3[:, c, :], in_=ps[:, r0:r0 + TQ]),
                         reads=banks_of(r0, r0 + TQ), writes=[qT.buf], name="evac_q")

            if ch == 0:
                dbg("kT", kT.ap, [kT.buf], BF16)
                dbg("vt", vt.ap, [vt.buf], BF16)
                dbg("qT", qT.ap, [qT.buf], BF16)
            qblocks = [(1, 127, 1, 0)] + [(eb, 0, 128, (eb - 2) * 128 + 1) for eb in range(2, NT + 2)] + \
                      [(NT + 2, 0, 1, TQ - 1)]
            brel4 = brel.ap.rearrange("p (j h q) -> p j h q", j=3, h=8)
            esink3 = esink.ap.rearrange("p (g q) -> p g q", g=4)
            ttc = {"i": 0}

            def at_qk(m):
                b, j = m // 3, m % 3
                eb, ql0, qn, qc0 = qblocks[b]
                n4 = 4 * qn
                kb = eb - 1 + j
                pr = m % 2
                for kvh in range(2):
                    p0 = kvh * 64
                    bk = 2 * pr + kvh
                    S.op("pe", lambda e, bk=bk, p0=p0: e.matmul(
                        out=ps[:, bk * 512:bk * 512 + n4].rearrange("p (g q) -> p g q", g=4),
                        lhsT=kT.ap[p0:p0 + 64, kb * 128:(kb + 1) * 128],
                        rhs=qT3[p0:p0 + 64, :, qc0:qc0 + qn], start=True, stop=True),
                        reads=[kT.buf, qT.buf], writes=[bank[bk]], name="qk")

            def at_exp(m):
                b, j = m // 3, m % 3
                eb, ql0, qn, qc0 = qblocks[b]
                n4 = 4 * qn
                kb = eb - 1 + j
                pr = m % 2
                tb = tt[m % 2]
                pt = PTr[m % 3]
                S.op("dve", lambda e: e.scalar_tensor_tensor(
                    out=tb.ap[:, 0:2 * n4].rearrange("p (k g q) -> p k g q", k=2, g=4),
                    in0=ps[:, 2 * pr * 512:(2 * pr + 2) * 512].rearrange("p (k c) -> p k c", k=2)[:, :, 0:n4]
                    .rearrange("p k (g q) -> p k g q", g=4),
                    scalar=0.125,
                    in1=brel4[:, j, :, ql0:ql0 + qn].rearrange("p (k g) q -> p k g q", k=2),
                    op0=ALU.mult, op1=ALU.add),
                    reads=[bank[2 * pr], bank[2 * pr + 1], brel.buf], writes=[tb.buf], name="sbias")
                S.op("act", lambda e: e.activation(
                    out=pt.ap[:, 0:2 * n4], in_=tb.ap[:, 0:2 * n4], func=AF.Exp,
                    bias=kbias.ap[:, blk0 + kb:blk0 + kb + 1], scale=1.0),
                    reads=[tb.buf, kbias.buf], writes=[pt.buf], name="exp")

            def at_pv(m):
                b, j = m // 3, m % 3
                eb, ql0, qn, qc0 = qblocks[b]
                i = b % 2
                n4 = 4 * qn
                kb = eb - 1 + j
                pt = PTr[m % 3]
                bo, bd = 4 + 2 * i, 5 + 2 * i
                for kvh in range(2):
                    p0 = kvh * 64
                    S.op("pe", lambda e, p0=p0, kvh=kvh: e.matmul(
                        out=ps[p0:p0 + 64, bo * 512:bo * 512 + n4], lhsT=vt3[:, kb, p0:p0 + 64],
                        rhs=pt.ap[:, kvh * n4:(kvh + 1) * n4], start=(j == 0), stop=(j == 2)),
                        reads=[vt.buf, pt.buf], writes=[bank[bo]], name="pv")
                for kvh in range(2):
                    p0 = kvh * 64
                    S.op("pe", lambda e, p0=p0, kvh=kvh: e.matmul(
                        out=ps[p0:p0 + 64, bd * 512:bd * 512 + n4], lhsT=ones.ap,
                        rhs=pt.ap[:, kvh * n4:(kvh + 1) * n4], start=(j == 0), stop=(j == 2)),
                        reads=[ones.buf, pt.buf], writes=[bank[bd]], name="den")

            def at_norm(m):
                b, j = m // 3, m % 3
                if j != 2:
                    return
                eb, ql0, qn, qc0 = qblocks[b]
                i = b % 2
                n4 = 4 * qn
                bo, bd = 4 + 2 * i, 5 + 2 * i
                S.op("dve", lambda e: e.tensor_tensor(
                    out=dtmp[i].ap[:, 0:n4].rearrange("p (g q) -> p g q", g=4),
                    in0=ps[:, bd * 512:bd * 512 + n4].rearrange("p (g q) -> p g q", g=4),
                    in1=esink3[:, :, ql0:ql0 + qn], op=ALU.add),
                    reads=[bank[bd], esink.buf], writes=[dtmp[i].buf], name="den_sink")
                S.op("act", lambda e: e.activation(out=dtmp[i].ap[:, 0:n4], in_=dtmp[i].ap[:, 0:n4], func=AF.Ln),
                     reads=[dtmp[i].buf], writes=[dtmp[i].buf], name="ln_den")
                S.op("act", lambda e: e.activation(out=dtmp[i].ap[:, 0:n4], in_=dtmp[i].ap[:, 0:n4], func=AF.Exp, scale=-1.0),
                     reads=[dtmp[i].buf], writes=[dtmp[i].buf], name="recip_den")

            def at_norm2(m):
                b, j = m // 3, m % 3
                if j != 2:
                    return
                eb, ql0, qn, qc0 = qblocks[b]
                i = b % 2
                n4 = 4 * qn
                bo, bd = 4 + 2 * i, 5 + 2 * i
                S.op("dve", lambda e: e.tensor_tensor(
                    out=attT3[:, :, qc0:qc0 + qn],
                    in0=ps[:, bo * 512:bo * 512 + n4].rearrange("p (g q) -> p g q", g=4),
                    in1=dtmp[i].ap[:, 0:n4].rearrange("p (g q) -> p g q", g=4), op=ALU.mult),
                    reads=[bank[bo], dtmp[i].buf], writes=[attT.buf], name="att_norm")

            pipeline(3 * len(qblocks), [at_qk, at_exp, at_pv, at_norm, at_norm2])

            wo = T(f"wo{ch}", SB, 8 * 1024, BF16)
            wo3 = wo.ap.rearrange("p (k f) -> p k f", k=8)
            S.op("pool", lambda e: e.dma_start(out=wo3, in_=wo_d, max_dma_last_dim=8192), writes=[wo.buf],
                 lane=lane_wo, name="ld_wo")

            if ch == 0:
                dbg("attT", attT.ap, [attT.buf], BF16)
            ccs = T(f"ccs{ch}", SC, TP, F32)
            pp = T(f"pp{ch}", SC + 1032, TP, F32)
            acc = T(f"acc{ch}", SC + 2064, TQ, F32)
            for j in range(4):
                wt, w3 = load_w(win_d[6 + 3 * j + 1])
                rc = region()
                proj_fm(w3, wt.buf, hT3, hT.cols, QLO - 1, TP, rc)
                S.op("act", lambda e, rc=rc: e.activation(out=ccs.ap, in_=ps[:, rc:rc + TP], func=AF.Copy),
                     reads=banks_of(rc, rc + TP), writes=[ccs.buf], name="evac_cc")
                wt, w3 = load_w(win_d[6 + 3 * j + 2])
                rx = region()
                proj_fm(w3, wt.buf, hT3, hT.cols, QLO - 1, TP, rx)
                S.op("dve", lambda e, rx=rx: e.tensor_tensor(out=pp.ap, in0=ps[:, rx:rx + TP], in1=ccs.ap, op=ALU.mult),
                     reads=banks_of(rx, rx + TP) + [ccs.buf], writes=[pp.buf], name="p_mul")
                wt, w3 = load_w(win_d[6 + 3 * j + 0])
                rb = rx
                proj_fm(w3, wt.buf, hT3, hT.cols, QLO, TQ, rb)
                S.op("act", lambda e, j=j: e.activation(out=acc.ap, in_=pp.ap[:, 1:1 + TQ], func=AF.Identity,
                                                        bias=cbs.ap[:, j:j + 1], scale=cw.ap[:, 3 * j + 1:3 * j + 2]),
                     reads=[pp.buf, cbs.buf, cw.buf], writes=[acc.buf], name="conv1")
                S.op("dve", lambda e, j=j: e.scalar_tensor_tensor(out=acc.ap, in0=pp.ap[:, 0:TQ],
                                                                  scalar=cw.ap[:, 3 * j:3 * j + 1], in1=acc.ap,
                                                                  op0=ALU.mult, op1=ALU.add),
                     reads=[pp.buf, cw.buf, acc.buf], writes=[acc.buf], name="conv0")
                S.op("dve", lambda e, j=j: e.scalar_tensor_tensor(out=acc.ap, in0=pp.ap[:, 2:2 + TQ],
                                                                  scalar=cw.ap[:, 3 * j + 2:3 * j + 3], in1=acc.ap,
                                                                  op0=ALU.mult, op1=ALU.add),
                     reads=[pp.buf, cw.buf, acc.buf], writes=[acc.buf], name="conv2")
                S.op("dve", lambda e, j=j, rb=rb: e.tensor_tensor(out=convT3[:, j, :], in0=ps[:, rb:rb + TQ],
                                                                  in1=acc.ap, op=ALU.mult),
                     reads=banks_of(rb, rb + TQ) + [acc.buf], writes=[convT.buf], name="conv_gate")

            if ch == 0:
                dbg("convT", convT.ap, [convT.buf], BF16)
            A2 = Alloc(SA, SA + 7184)
            gab = [T(f"ga{ch}_{i}", A2.take(TQ), TQ, F32) for i in range(2)]
            gcb = [T(f"gc{ch}_{i}", A2.take(TQ), TQ, F32) for i in range(2)]
            t1b = [T(f"t1{ch}_{i}", A2.take(TQ), TQ, F32) for i in range(2)]
            for f in range(8):
                i = f % 2
                wt, w3 = load_w(win_d[18 + 2 * f])
                ra = region()
                proj_fm(w3, wt.buf, hT3, hT.cols, QLO, TQ, ra)
                S.op("act", lambda e, ra=ra, f=f, i=i: e.activation(out=gab[i].ap, in_=ps[:, ra:ra + TQ], func=AF.Sigmoid,
                                                                    bias=bg.ap[:, f:f + 1], scale=1.0),
                     reads=banks_of(ra, ra + TQ) + [bg.buf], writes=[gab[i].buf], name="sig_a")
                wt, w3 = load_w(win_d[18 + 2 * f + 1])
                rc = region()
                proj_fm(w3, wt.buf, hT3, hT.cols, QLO, TQ, rc)
                S.op("act", lambda e, rc=rc, f=f, i=i: e.activation(out=gcb[i].ap, in_=ps[:, rc:rc + TQ], func=AF.Sigmoid,
                                                                    bias=bg.ap[:, 8 + f:9 + f], scale=1.0),
                     reads=banks_of(rc, rc + TQ) + [bg.buf], writes=[gcb[i].buf], name="sig_c")
                wt, w3 = load_w(wa_d[f], nk=4)
                ra = region()
                proj_fm(w3, wt.buf, attT3, [attT.buf], 0, TQ, ra, nk=4)
                S.op("dve", lambda e, ra=ra, i=i: e.tensor_tensor(out=t1b[i].ap, in0=ps[:, ra:ra + TQ], in1=gab[i].ap,
                                                                  op=ALU.mult),
                     reads=banks_of(ra, ra + TQ) + [gab[i].buf], writes=[t1b[i].buf], name="gate_a")
                wt, w3 = load_w(wc_d[f], nk=4)
                rc = region()
                proj_fm(w3, wt.buf, convT3, [convT.buf], 0, TQ, rc, nk=4)
                S.op("dve", lambda e, rc=rc, i=i: e.tensor_tensor(out=gcb[i].ap, in0=ps[:, rc:rc + TQ], in1=gcb[i].ap,
                                                                  op=ALU.mult),
                     reads=banks_of(rc, rc + TQ) + [gcb[i].buf], writes=[gcb[i].buf], name="gate_c")
                S.op("dve", lambda e, f=f, i=i: e.tensor_tensor(out=mT[f].ap[:, QLO - 128:QLO - 128 + TQ], in0=t1b[i].ap,
                                                                 in1=gcb[i].ap, op=ALU.add),
                     reads=[t1b[i].buf, gcb[i].buf], writes=[mT[f].buf], name="merge")

            if ch == 0:
                dbg("mT0", mT[0].ap, [mT[0].buf], BF16)
                dbg("mT7", mT[7].ap, [mT[7].buf], BF16)
            A3 = Alloc(SA, SA + 7184)
            tmp2 = [T(f"tmp2{ch}_{i}", A3.take(1024), 1024, F32) for i in range(4)]
            hb2 = [T(f"hb2{ch}_{i}", A3.take(512), 1024, BF16) for i in range(2)]
            junk2 = T(f"junk2{ch}", A3.take(512), 1024, BF16)
            h2T = TiledT(f"h2T{ch}", hT_lo)
            h2T3 = h2T.ap.rearrange("p (k f) -> p k f", k=8)
            ln_load_gb(1)

            p2_order = [0, NT + 1] + list(range(1, NT + 1))

            def p2_mm(n):
                t = p2_order[n]
                r0 = (n % 2) * 1024
                for half in range(2):
                    for k in range(8):
                        S.op("pe", lambda e, k=k, half=half: e.matmul(
                            out=ps[:, r0 + half * 512:r0 + (half + 1) * 512], lhsT=mT[k].ap[:, t * 128:(t + 1) * 128],
                            rhs=wo3[:, k, half * 512:(half + 1) * 512], start=(k == 0), stop=(k == 7)),
                            reads=[mT[k].buf, wo.buf], writes=[bank[r0 // 512 + half]], name="wo_mm")

            def p2_resid(n):
                t = p2_order[n]
                r0 = (n % 2) * 1024
                i = n % 4
                S.op("dve", lambda e: e.scalar_tensor_tensor(
                    out=tmp2[i].ap, in0=hres[t].ap, scalar=ALPHA, in1=ps[:, r0:r0 + 1024], op0=ALU.mult, op1=ALU.add),
                    reads=[hres[t].buf] + banks_of(r0, r0 + 1024), writes=[tmp2[i].buf], name="resid1")

            def p2_norm(n):
                t = p2_order[n]
                ln_norm(tmp2[n % 4], hres[t], stat[n % 3], hb=hb2[n % 2],
                        valid_ap=tokvalid.ap[:, blk0 + t + 1:blk0 + t + 2])

            pipeline(NT + 2, [
                p2_mm,
                p2_resid,
                lambda n: ln_stats_a(tmp2[n % 4], stat[n % 3], ps[:, 4 * 512:6 * 512], [bank[4], bank[5]]),
                lambda n: ln_stats(tmp2[n % 4], stat[n % 3], from_act=True),
                p2_norm,
                lambda n: tr_mm(hb2[n % 2], 6 + n % 2),
                lambda n: tr_evac(h2T.ap, h2T.bufs[p2_order[n] + 1], (p2_order[n] + 1) * 128, 6 + n % 2),
            ])

            if ch == 0:
                dbg("h2_t1", hres[1].ap, [hres[1].buf])
                dbg("h2T", h2T.ap, h2T.bufs[1:NT + 3], BF16)
            OB = Alloc(OV, AW)
            gT = [T(f"gT{ch}_{j}", OB.take(512), 1024, BF16) for j in range(NJ)]
            wd_lo = OB.take(NJ * 512)
            accs = [[T(f"facc{ch}_{i}_{s}", OB.take(1024), 1024, F32) for s in range(2)] for i in range(2)]
            o_lo = accs[0][0].buf.lo
            wdt = T(f"wd{ch}", wd_lo, NJ * 1024, BF16)
            wd3 = wdt.ap.rearrange("p (k f) -> p k f", k=NJ)
            for j in range(NJ):
                i = j % 2
                S.op("pool", lambda e, j=j: e.dma_start(out=wd3[:, j, :], in_=wd_d[:, j, :], max_dma_last_dim=8192),
                     writes=[wdt.buf], lane=lane_wd, name="ld_wd")
                for s in range(2):
                    ci = 2 * j + s
                    wt, w3 = load_w(wup_d[ci])
                    r0 = (ci % 3) * 1024
                    hbk = 6 + ci % 2
                    h0 = hbk * 512 + 2 * (ci // 2 % 64)
                    proj_fm(w3, wt.buf, h2T3, h2T.cols, QLO + 1, TC, r0)
                    for k in range(8):
                        S.op("pe", lambda e, k=k, w3=w3, h0=h0: e.matmul(
                            out=ps[:, h0:h0 + 2], lhsT=w3[:, k, :], rhs=h2T3[:, k, QLO:QLO + TQ:TQ - 1],
                            start=(k == 0), stop=(k == 7)),
                            reads=[wt.buf, h2T.bufs[1], h2T.bufs[NT + 2]], writes=[bank[hbk]], name="proj_halo")
                    a = accs[i][s]
                    rb = banks_of(r0, r0 + TC)
                    S.op("act", lambda e, r0=r0, a=a, ci=ci: e.activation(
                        out=a.ap, in_=ps[:, r0:r0 + TC], func=AF.Identity, bias=fcb.ap[:, ci:ci + 1],
                        scale=fcw.ap[:, 3 * ci + 1:3 * ci + 2]),
                        reads=rb + [fcb.buf, fcw.buf], writes=[a.buf], name="fconv1")
                    S.op("act", lambda e, a=a, ci=ci, h0=h0: e.activation(
                        out=a.ap[:, 0:1], in_=ps[:, h0:h0 + 1], func=AF.Identity, bias=a.ap[:, 0:1],
                        scale=fcw.ap[:, 3 * ci:3 * ci + 1]),
                        reads=[bank[hbk], fcw.buf, a.buf], writes=[a.buf], name="fedgeL")
                    S.op("act", lambda e, a=a, ci=ci, h0=h0: e.activation(
                        out=a.ap[:, TC - 1:TC], in_=ps[:, h0 + 1:h0 + 2], func=AF.Identity, bias=a.ap[:, TC - 1:TC],
                        scale=fcw.ap[:, 3 * ci + 2:3 * ci + 3]),
                        reads=[bank[hbk], fcw.buf, a.buf], writes=[a.buf], name="fedgeR")
                    S.op("dve", lambda e, r0=r0, a=a, ci=ci: e.scalar_tensor_tensor(
                        out=a.ap[:, 1:TC], in0=ps[:, r0:r0 + TC - 1], scalar=fcw.ap[:, 3 * ci:3 * ci + 1],
                        in1=a.ap[:, 1:TC], op0=ALU.mult, op1=ALU.add),
                        reads=rb + [fcw.buf, a.buf], writes=[a.buf], name="fconv0")
                    S.op("dve", lambda e, r0=r0, a=a, ci=ci: e.scalar_tensor_tensor(
                        out=a.ap[:, 0:TC - 1], in0=ps[:, r0 + 1:r0 + TC], scalar=fcw.ap[:, 3 * ci + 2:3 * ci + 3],
                        in1=a.ap[:, 0:TC - 1], op0=ALU.mult, op1=ALU.add),
                        reads=rb + [fcw.buf, a.buf], writes=[a.buf], name="fconv2")
                aa, uu = accs[i][0], accs[i][1]
                S.op("act", lambda e, aa=aa: e.activation(out=aa.ap, in_=aa.ap, func=AF.Silu),
                     reads=[aa.buf], writes=[aa.buf], name="silu")
                S.op("dve", lambda e, aa=aa, uu=uu, j=j: e.tensor_tensor(out=gT[j].ap, in0=aa.ap, in1=uu.ap, op=ALU.mult),
                     reads=[aa.buf, uu.buf], writes=[gT[j].buf], name="glu")
            if ch == 0:
                dbg("gT0", gT[0].ap, [gT[0].buf], BF16)
                dbg("gT21", gT[21].ap, [gT[21].buf], BF16)
            OT = Alloc(o_lo, AW)
            tmp4 = [T(f"tmp4{ch}_{i}", OT.take(1024), 1024, F32) for i in range(2)]
            ot = [T(f"ot{ch}_{i}", OT.take(1024), 1024, F32) for i in range(2)]
            ln_load_gb(2)
            out_ops = []

            def p4_mm(t):
                r0 = (t % 2) * 1024
                for half in range(2):
                    for k in range(NJ):
                        S.op("pe", lambda e, k=k, half=half: e.matmul(
                            out=ps[:, r0 + half * 512:r0 + (half + 1) * 512], lhsT=gT[k].ap[:, t * 128:(t + 1) * 128],
                            rhs=wd3[:, k, half * 512:(half + 1) * 512], start=(k == 0), stop=(k == NJ - 1)),
                            reads=[gT[k].buf, wdt.buf], writes=[bank[r0 // 512 + half]], name="wd_mm")

            act4 = (ch < NCH - 1)
            t4 = tmp4 + ot

            def p4_resid(t):
                r0 = (t % 2) * 1024
                src = t4[t % 4] if act4 else tmp4[t % 2]
                S.op("dve", lambda e: e.scalar_tensor_tensor(
                    out=src.ap, in0=hres[t + 1].ap, scalar=ALPHA, in1=ps[:, r0:r0 + 1024], op0=ALU.mult, op1=ALU.add),
                    reads=[hres[t + 1].buf] + banks_of(r0, r0 + 1024), writes=[src.buf], name="resid2")

            def p4_out(t, src):
                orow = ch * TC + t * 128
                o = S.op("sp", lambda e: e.dma_start(out=out_d[orow:orow + 128, :], in_=src.ap),
                         reads=[src.buf], lane=lane_o[t % 4 if act4 else t % 2], name="st_out")
                out_ops.append(o)

            if act4:
                junk4 = ps[:, 4 * 512:6 * 512]

                def p4_norm(t):
                    ln_norm(t4[t % 4], t4[t % 4], stat[t % 3])
                    p4_out(t, t4[t % 4])

                yield Pipe(NT, [
                    p4_mm,
                    p4_resid,
                    lambda t: ln_stats_a(t4[t % 4], stat[t % 3], junk4, [bank[4], bank[5]]),
                    lambda t: ln_stats(t4[t % 4], stat[t % 3], from_act=True),
                    p4_norm,
                ])
            else:
                def p4_stats(t):
                    p4_resid(t)
                    ln_stats(tmp4[t % 2], stat[t % 3])

                def p4_norm(t):
                    i = t % 2
                    ln_norm(tmp4[i], ot[i], stat[t % 3])
                    p4_out(t, ot[i])

                yield Pipe(NT, [p4_mm, p4_stats, p4_norm])
            out_ops_all.extend(out_ops)
        out_ops_all = []
        g0 = do_chunk(0)
        next(g0).run()
        p4 = next(g0)
        g1 = do_chunk(1)
        p0n = next(g1)
        j = 0
        for sidx in range(p4.nsteps):
            p4.step(sidx)
            for _ in range(2):
                if j < p0n.nsteps and j <= sidx + 6:
                    p0n.step(j)
                    j += 1
        while j < p0n.nsteps:
            p0n.step(j)
            j += 1
        for _ in g0:
            pass
        p4b = next(g1)
        p4b.run()
        for _ in g1:
            pass
        final = out_ops_all[-4:]

        S.emit(nc, es, final + dbg_list)
    return nc


def _rel_bucket(rel):
    half = 16
    max_exact = 8
    offset = np.where(rel > 0, half, 0)
    n = np.abs(rel)
    nf = np.maximum(n, 1).astype(np.float32)
    large = max_exact + (np.log(nf / max_exact) / math.log(128 / max_exact) * (half - max_exact)).astype(np.int32)
    large = np.minimum(large, half - 1)
    return offset + np.where(n < max_exact, n, large)


def _tile_w(w, col_idx):
    sub = w[:, col_idx]
    K = sub.shape[0]
    return np.ascontiguousarray(sub.reshape(K // 128, 128, -1).transpose(1, 0, 2))


_PROGRAM = None


def kernel(x, ln_in_g, ln_in_b, w_in, b_gates, attn_sink, rel_bias, conv_w, conv_b,
           w_att_branch, w_conv_branch, w_o, ln_mix_g, ln_mix_b,
           w_ffn_up, ffn_conv_w, ffn_conv_b, w_ffn_down, ln_ffn_g, ln_ffn_b):
    global _PROGRAM
    f32 = np.float32
    x = np.asarray(x, f32)
    w_in0 = np.asarray(w_in, f32)[0]
    ar = np.arange(128)
    cols = []
    cols.append(512 + ar)
    cols.append(640 + ar)
    for c in range(4):
        cols.append(np.concatenate([c * 64 + np.arange(64), (4 + c) * 64 + np.arange(64)]))
    for j in range(4):
        cols.append(768 + j * 128 + ar)
        cols.append(1280 + j * 128 + ar)
        cols.append(1792 + j * 128 + ar)
    for f in range(8):
        cols.append(2304 + f * 128 + ar)
        cols.append(3328 + f * 128 + ar)
    win = np.stack([_tile_w(w_in0, c) for c in cols])
    bgv = np.asarray(b_gates, f32)[0]
    bg = np.ascontiguousarray(bgv.reshape(16, 128).T)
    cwv = np.asarray(conv_w, f32)[0]
    cw = np.ascontiguousarray(cwv.reshape(3, 4, 128).transpose(2, 1, 0).reshape(128, 12))
    cbias = np.ascontiguousarray(np.asarray(conv_b, f32)[0].reshape(4, 128).T)
    wab = np.asarray(w_att_branch, f32)[0]
    wa = np.ascontiguousarray(wab.reshape(2, 4, 64, 8, 128).transpose(3, 0, 2, 1, 4).reshape(8, 128, 4, 128))
    wcb = np.asarray(w_conv_branch, f32)[0]
    wc = np.ascontiguousarray(wcb.reshape(4, 128, 8, 128).transpose(2, 1, 0, 3))
    wo = np.ascontiguousarray(np.asarray(w_o, f32)[0].reshape(8, 128, 1024).transpose(1, 0, 2))
    wup0 = np.asarray(w_ffn_up, f32)[0]
    upcols = []
    for j in range(NJ):
        upcols.append(j * 128 + ar)
        upcols.append(DFF + j * 128 + ar)
    wup = np.stack([_tile_w(wup0, c) for c in upcols])
    fw = np.asarray(ffn_conv_w, f32)[0]
    fb = np.asarray(ffn_conv_b, f32)[0]
    fcw = np.ascontiguousarray(np.stack([fw[:, c] for c in upcols]).transpose(2, 0, 1).reshape(128, 2 * NJ * 3))
    fcb = np.ascontiguousarray(np.stack([fb[c] for c in upcols]).T)
    wd = np.ascontiguousarray(np.asarray(w_ffn_down, f32)[0].reshape(NJ, 128, 1024).transpose(1, 0, 2))
    c_i = np.arange(128)[:, None]
    q_i = np.arange(128)[None, :]
    rb = np.asarray(rel_bias, f32)
    brel = np.empty((128, 3, 8, 128), f32)
    for j in range(3):
        rel = (j - 1) * 128 + c_i - q_i
        bt = rb[_rel_bucket(rel)]
        bt = np.where((np.abs(rel) <= 128)[:, :, None], bt, f32(NEG))
        brel[:, j] = bt.transpose(0, 2, 1)
    brel = np.ascontiguousarray(brel.reshape(128, 3 * 8 * 128))
    sk = np.asarray(attn_sink, f32)[0].reshape(2, 1, 4, 1)
    sinkx = np.ascontiguousarray(np.broadcast_to(sk, (2, 64, 4, 128)).reshape(128, 512))

    shared = {
        "win": win, "bg": bg, "cw": cw, "cbias": cbias, "wa": wa, "wc": wc, "wo": wo,
        "ln_in_g": np.asarray(ln_in_g, f32).reshape(1, D), "ln_in_b": np.asarray(ln_in_b, f32).reshape(1, D),
        "ln_mix_g": np.asarray(ln_mix_g, f32).reshape(1, D), "ln_mix_b": np.asarray(ln_mix_b, f32).reshape(1, D),
        "ln_ffn_g": np.asarray(ln_ffn_g, f32).reshape(1, D), "ln_ffn_b": np.asarray(ln_ffn_b, f32).reshape(1, D),
        "wup": wup, "fcw": fcw, "fcb": fcb, "wd": wd, "brel": brel, "sinkx": sinkx,
    }
    in_maps = []
    for c in range(NCORES):
        b, half = c // 2, c % 2
        s = half * TOK_CORE
        xe = np.zeros((XROWS, D), f32)
        lo, hi = s - 256, s + TOK_CORE + 256
        slo, shi = max(lo, 0), min(hi, SEQ)
        xe[slo - lo:shi - lo] = x[b, slo:shi]
        tok = lo + np.arange(XROWS)
        valid = ((tok >= 0) & (tok < SEQ)).astype(f32).reshape(NBLK_CORE, 128).T
        m = dict(shared)
        m["x"] = xe
        m["tokvalid"] = np.ascontiguousarray(valid)
        m["kbias"] = np.ascontiguousarray(np.where(valid > 0, f32(0.0), f32(NEG)).astype(f32))
        in_maps.append(m)

    if _PROGRAM is None:
        _PROGRAM = build_program()
    res = run_bass_kernel_spmd(_PROGRAM, in_maps, core_ids=list(range(NCORES)))
    if DEBUG:
        global _LAST
        _LAST = res.results
    out = np.empty((4, SEQ, D), f32)
    for c in range(NCORES):
        b, half = c // 2, c % 2
        out[b, half * TOK_CORE:(half + 1) * TOK_CORE] = res.results[c]["out"]
    return out
```

```python
import math
from contextlib import ExitStack

import numpy as np
import concourse.bass as bass
import concourse.mybir as mybir
from concourse.bass_utils import run_bass_kernel_spmd

F32 = mybir.dt.float32
BF16 = mybir.dt.bfloat16
AF = mybir.ActivationFunctionType
ALU = mybir.AluOpType

D = 1024
SEQ = 4096
NCORES = 8
TOK_CORE = 2048
NCH = 2
TC = TOK_CORE // NCH
NT = TC // 128
NBE = NT + 4
TE = NBE * 128
TQ = TC + 2
TP = TC + 4
QLO = 255
XROWS = TOK_CORE + 512
NBLK_CORE = XROWS // 128
DFF = 2816
NJ = DFF // 128
ALPHA = 2.0 ** 0.25
EPS = 1e-5
NEG = -30000.0
NSLOT = 6
DEBUG = False
N_INCH = 34

ENGS = ("pe", "act", "dve", "pool", "sp")


class Lane:
    def __init__(self, sem):
        self.sem = sem
        self.count = 0


class Op:
    __slots__ = ("eng", "fn", "deps", "marked", "val", "lane", "name")

    def __init__(self, eng, fn, lane=None, name=""):
        self.eng = eng
        self.fn = fn
        self.deps = set()
        self.marked = False
        self.val = None
        self.lane = lane
        self.name = name


class Buf:
    def __init__(self, name, lo=None, hi=None):
        self.name = name
        self.last_w = None
        self.readers = []
        self.lo = lo
        self.hi = hi
        self.dead = False
        self.strict = False


class Sched:
    def __init__(self):
        self.ops = {e: [] for e in ENGS}
        self.all_bufs = []

    def buf(self, name, lo=None, hi=None):
        b = Buf(name, lo, hi)
        if lo is not None:
            for o in self.all_bufs:
                if o.lo is not None and o.lo < hi and lo < o.hi:
                    if o.last_w is not None:
                        b.readers.append(o.last_w)
                    b.readers.extend(o.readers)
                    o.dead = True
            self.all_bufs.append(b)
        return b

    def buf_group(self, names, lo, hi):
        inherited = []
        for o in self.all_bufs:
            if o.lo is not None and o.lo < hi and lo < o.hi:
                if o.last_w is not None:
                    inherited.append(o.last_w)
                inherited.extend(o.readers)
                o.dead = True
        out = []
        for n in names:
            b = Buf(n, lo, hi)
            b.readers = list(inherited)
            out.append(b)
        self.all_bufs.extend(out)
        return out

    def op(self, eng, fn, reads=(), writes=(), lane=None, name=""):
        op = Op(eng, fn, lane, name)
        is_dma = lane is not None

        def same(o):
            return (not is_dma) and (o.lane is None) and o.eng == eng

        for b in reads:
            assert not b.dead, f"read of dead buf {b.name} in {name}"
            w = b.last_w
            if w is not None and not (same(w) and eng == "pe"):
                op.deps.add(w)
        for b in writes:
            assert not b.dead, f"write of dead buf {b.name} in {name}"
            w = b.last_w
            if w is not None and (not same(w) or eng != "pe"):
                op.deps.add(w)
            for r in b.readers:
                if not same(r) or eng != "pe":
                    op.deps.add(r)
        for b in reads:
            b.readers.append(op)
        for b in writes:
            b.last_w = op
            b.readers = []
        op.deps.discard(op)
        self.ops[eng].append(op)
        return op

    def emit(self, nc, es, final_waits):
        for e in ENGS:
            for op in self.ops[e]:
                for d in op.deps:
                    d.marked = True
        for d in final_waits:
            d.marked = True
        esem = {e: es.enter_context(nc.semaphore("eng_" + e)) for e in ENGS}
        cnt = {e: 0 for e in ENGS}
        for e in ENGS:
            for op in self.ops[e]:
                if op.lane is not None:
                    op.lane.count += 16
                    op.val = (op.lane.sem, op.lane.count)
                elif op.marked:
                    cnt[e] += 1
                    op.val = (esem[e], cnt[e])
        block = es.enter_context(nc.Block())

        def run(e, eng):
            seen = {}
            for op in self.ops[e]:
                need = {}
                for d in op.deps:
                    sem, v = d.val
                    k = id(sem)
                    if seen.get(k, (None, 0))[1] >= v:
                        continue
                    if k not in need or need[k][1] < v:
                        need[k] = (sem, v)
                for k, (sem, v) in need.items():
                    eng.wait_ge(sem, v)
                    seen[k] = (sem, v)
                ins = op.fn(eng)
                if op.lane is not None:
                    ins.then_inc(op.lane.sem, 16)
                elif op.marked:
                    ins.then_inc(esem[e], 1)
            if e == "sp":
                need = {}
                for d in final_waits:
                    sem, v = d.val
                    k = id(sem)
                    if k not in need or need[k][1] < v:
                        need[k] = (sem, v)
                for k, (sem, v) in need.items():
                    eng.wait_ge(sem, v)

        block.tensor(lambda eng: run("pe", eng))
        block.scalar(lambda eng: run("act", eng))
        block.vector(lambda eng: run("dve", eng))
        block.gpsimd(lambda eng: run("pool", eng))
        block.sync(lambda eng: run("sp", eng))


def build_program():
    nc = bass.Bass("TRN2", target_bir_lowering=False)

    def din(name, shape):
        return nc.dram_tensor(name, list(shape), F32, kind="ExternalInput").ap()

    x_d = din("x", [XROWS, D])
    tokvalid_d = din("tokvalid", [128, NBLK_CORE])
    kbias_d = din("kbias", [128, NBLK_CORE])
    brel_d = din("brel", [128, 3 * 8 * 128])
    sinkx_d = din("sinkx", [128, 512])
    win_d = din("win", [N_INCH, 128, 8, 128])
    bg_d = din("bg", [128, 16])
    cw_d = din("cw", [128, 12])
    cb_d = din("cbias", [128, 4])
    wa_d = din("wa", [8, 128, 4, 128])
    wc_d = din("wc", [8, 128, 4, 128])
    wo_d = din("wo", [128, 8, 1024])
    lng_d = [din("ln_in_g", [1, D]), din("ln_mix_g", [1, D]), din("ln_ffn_g", [1, D])]
    lnb_d = [din("ln_in_b", [1, D]), din("ln_mix_b", [1, D]), din("ln_ffn_b", [1, D])]
    wup_d = din("wup", [2 * NJ, 128, 8, 128])
    fcw_d = din("fcw", [128, 2 * NJ * 3])
    fcb_d = din("fcb", [128, 2 * NJ])
    wd_d = din("wd", [128, NJ, 1024])
    out_d = nc.dram_tensor("out", [TOK_CORE, D], F32, kind="ExternalOutput").ap()

    S = Sched()
    es = ExitStack()
    dbg_list = []

    def dbg(name, ap, bufs, dt=F32):
        if not DEBUG:
            return
        shape = list(ap.shape)
        dd = nc.dram_tensor("dbg_" + name, shape, dt, kind="ExternalOutput").ap()
        ln = Lane(es.enter_context(nc.semaphore("l_dbg_" + name)))
        o = S.op("sp", lambda e: e.dma_start(out=dd, in_=ap), reads=bufs, lane=ln, name="dbg_" + name)
        dbg_list.append(o)
    with es:
        AW = 53200
        arena = es.enter_context(nc.sbuf_tensor("arena", [128, AW], F32))
        ps = es.enter_context(nc.psum_tensor("ps", [128, 4096], F32))
        es.enter_context(nc.allow_low_precision(reason="bf16 matmul operands, fp32 accumulation"))

        def new_lane(name):
            return Lane(es.enter_context(nc.semaphore(name)))

        class Alloc:
            def __init__(self, lo, hi):
                self.p = lo
                self.hi = hi

            def take(self, words):
                words = (words + 7) // 8 * 8
                lo = self.p
                self.p += words
                assert self.p <= self.hi, (self.p, self.hi)
                return lo

        def f32ap(lo, n, parts=128):
            return arena[0:parts, lo:lo + n]

        def b16ap(lo, n, parts=128):
            assert n % 2 == 0
            return arena[0:parts, lo:lo + n // 2].bitcast(BF16)

        class T:
            def __init__(self, name, lo, n, dt, parts=128):
                words = n if dt is F32 else n // 2
                self.buf = S.buf(name, lo, lo + words)
                self.ap = f32ap(lo, n, parts) if dt is F32 else b16ap(lo, n, parts)

        PA = Alloc(0, AW)
        hres = [T(f"hres{t}", PA.take(1024), 1024, F32) for t in range(NT + 2)]
        hT_lo = PA.take(8 * TE // 2)
        gtile = T("gtile", PA.take(1024), 1024, F32)
        btile = T("btile", PA.take(1024), 1024, F32)
        R_W = 7168
        R_lo = PA.take(R_W)
        cur = {"wslots": None, "wlimit": 0, "brel": None, "esink": None}
        ident = T("ident", PA.take(64), 128, BF16)
        ones = T("ones", PA.take(32), 64, BF16)
        tokvalid = T("tokvalid", PA.take(NBLK_CORE), NBLK_CORE, F32)
        kbias = T("kbias", PA.take(NBLK_CORE), NBLK_CORE, F32)
        bg = T("bg", PA.take(16), 16, F32)
        cw = T("cw", PA.take(12), 12, F32)
        cbs = T("cbs", PA.take(4), 4, F32)
        fcw = T("fcw", PA.take(6 * NJ), 6 * NJ, F32)
        fcb = T("fcb", PA.take(2 * NJ), 2 * NJ, F32)
        class Stat:
            def __init__(self, i):
                self.bn = T(f"stat_bn{i}", PA.take(16), 16, F32)
                self.mv = T(f"stat_mv{i}", PA.take(8), 8, F32)
                self.rs = T(f"stat_rs{i}", PA.take(8), 8, F32)

        stat = [Stat(i) for i in range(3)]
        stat0 = [Stat(i + 3) for i in range(3)]
        OV = PA.p
        OVW = AW - OV

        bank = [S.buf(f"bank{i}") for i in range(8)]
        bank[4].strict = True
        bank[5].strict = True

        def banks_of(c0, c1):
            return [bank[i] for i in range(c0 // 512, (c1 - 1) // 512 + 1)]

        lane_ws = [new_lane(f"l_ws{i}") for i in range(NSLOT)]
        lane_x = [new_lane(f"l_x{i}") for i in range(4)]
        lane_o = [new_lane(f"l_o{i}") for i in range(4)]
        lane_g = new_lane("l_g")
        lane_b = new_lane("l_b")
        lane_g0 = new_lane("l_g0")
        lane_b0 = new_lane("l_b0")
        lane_wo = new_lane("l_wo")
        lane_wd = new_lane("l_wd")

        def const_load(t, src, eng="sp"):
            ln = new_lane("l_c_" + t.buf.name)
            S.op(eng, lambda e, t=t, src=src: e.dma_start(out=t.ap, in_=src), writes=[t.buf], lane=ln,
                 name="const_" + t.buf.name)

        const_load(tokvalid, tokvalid_d)
        S.op("pool", lambda e: e.memset(ident.ap, 0.0), writes=[ident.buf], name="ident0")
        S.op("pool", lambda e: e.affine_select(out=ident.ap, in_=ident.ap, pattern=[[-1, 128]],
                                               compare_op=ALU.not_equal, fill=1.0, base=0, channel_multiplier=1),
             reads=[ident.buf], writes=[ident.buf], name="ident1")
        S.op("pool", lambda e: e.memset(ones.ap, 1.0), writes=[ones.buf], name="ones")

        def late_consts():
            const_load(kbias, kbias_d)
            const_load(bg, bg_d)
            const_load(cw, cw_d)
            const_load(cbs, cb_d)
            const_load(fcw, fcw_d)
            const_load(fcb, fcb_d)

        def region_b(ch):
            ab = Alloc(R_lo, R_lo + R_W)
            cur["wslots"] = [T(f"wslot{ch}_{i}", ab.take(512), 1024, BF16) for i in range(NSLOT)]
            cur["brel"] = T(f"brel{ch}", ab.take(3072), 3072, F32)
            cur["esink"] = T(f"esink{ch}", ab.take(512), 512, F32)
            cur["wlimit"] = (ch + 1) * W_PER_CHUNK
            brel, esink = cur["brel"], cur["esink"]
            const_load(brel, brel_d)
            const_load(esink, sinkx_d)
            S.op("act", lambda e: e.activation(out=esink.ap, in_=esink.ap, func=AF.Exp), reads=[esink.buf],
                 writes=[esink.buf], name="esink_exp")

        wseq = []
        for _ch in range(NCH):
            wseq += [(win_d[0], 128, 8), (win_d[1], 128, 8)] + [(win_d[2 + c], 128, 8) for c in range(4)]
            for j in range(4):
                wseq += [(win_d[6 + 3 * j + 1], 128, 8), (win_d[6 + 3 * j + 2], 128, 8), (win_d[6 + 3 * j], 128, 8)]
            for f in range(8):
                wseq += [(win_d[18 + 2 * f], 128, 8), (win_d[18 + 2 * f + 1], 128, 8), (wa_d[f], 128, 4), (wc_d[f], 128, 4)]
            wseq += [(wup_d[ci], 128, 8) for ci in range(2 * NJ)]
        ring = {"cur": 0, "issued": 0}
        W_PER_CHUNK = len(wseq) // NCH

        def load_w(src_ap=None, parts=128, nk=8):
            n = ring["cur"]
            ring["cur"] += 1
            assert wseq[n][1] == parts and wseq[n][2] == nk, (n, wseq[n][1:], parts, nk)
            wslots = cur["wslots"]
            while ring["issued"] < min(len(wseq), n + NSLOT, cur["wlimit"]):
                m = ring["issued"]
                ring["issued"] += 1
                src, mp, mk = wseq[m]
                tm = wslots[m % NSLOT]
                dst = tm.ap[0:mp, 0:mk * 128].rearrange("p (k f) -> p k f", k=mk)
                S.op("pool", lambda e, dst=dst, src=src: e.dma_start(out=dst, in_=src, max_dma_last_dim=8192),
                     writes=[tm.buf], lane=lane_ws[m % NSLOT], name="wload")
            t = wslots[n % NSLOT]
            return t, t.ap.rearrange("p (k f) -> p k f", k=8)

        rr = {"i": 0}

        def region():
            r = rr["i"] % 2
            rr["i"] += 1
            return r * 1536

        def evac_engine():
            return "act"

        def ln_load_gb(which):
            S.op("sp", lambda e: e.dma_start(out=gtile.ap, in_=lng_d[which].partition_broadcast(128)),
                 writes=[gtile.buf], lane=lane_g, name="ld_g")
            S.op("sp", lambda e: e.dma_start(out=btile.ap, in_=lnb_d[which].partition_broadcast(128)),
                 writes=[btile.buf], lane=lane_b, name="ld_b")

        class Pipe:
            def __init__(self, n_items, stages):
                self.n = n_items
                self.stages = stages
                self.nsteps = n_items + len(stages) - 1

            def step(self, s):
                for k, st in enumerate(self.stages):
                    t = s - k
                    if 0 <= t < self.n:
                        st(t)

            def run(self):
                for s in range(self.nsteps):
                    self.step(s)

        def pipeline(n_items, stages):
            Pipe(n_items, stages).run()

        def ln_stats_a(src, st, junk_ap, junk_buf):
            sa = st.bn.ap
            jb = junk_buf if isinstance(junk_buf, list) else [junk_buf]
            S.op("act", lambda e: e.activation(out=junk_ap, in_=src.ap, func=AF.Identity, accum_out=sa[:, 0:1]),
                 reads=[src.buf], writes=[st.bn.buf] + jb, name="ln_sum")
            S.op("act", lambda e: e.activation(out=junk_ap, in_=src.ap, func=AF.Square, accum_out=sa[:, 1:2]),
                 reads=[src.buf, st.bn.buf], writes=[st.bn.buf] + jb, name="ln_sumsq")

        def ln_stats(src, st, from_act=False, gt=None):
            gt = gt or gtile
            sa = st.bn.ap
            mv = st.mv.ap
            if not from_act:
                S.op("dve", lambda e: e.bn_stats(out=sa[:, 0:6], in_=src.ap[:, 0:512]), reads=[src.buf], writes=[st.bn.buf])
                S.op("dve", lambda e: e.bn_stats(out=sa[:, 6:12], in_=src.ap[:, 512:1024]), reads=[src.buf, st.bn.buf],
                     writes=[st.bn.buf])
                S.op("dve", lambda e: e.bn_aggr(out=mv[:, 0:2], in_=sa[:, 0:12]), reads=[st.bn.buf], writes=[st.mv.buf])
            else:
                S.op("dve", lambda e: e.tensor_scalar(out=mv[:, 0:1], in0=sa[:, 0:1], scalar1=1.0 / D, scalar2=None,
                                                      op0=ALU.mult), reads=[st.bn.buf], writes=[st.mv.buf])
                S.op("dve", lambda e: e.tensor_scalar(out=sa[:, 2:3], in0=mv[:, 0:1], scalar1=mv[:, 0:1], scalar2=-1.0,
                                                      op0=ALU.mult, op1=ALU.mult), reads=[st.mv.buf], writes=[st.bn.buf])
                S.op("dve", lambda e: e.tensor_scalar(out=mv[:, 1:2], in0=sa[:, 1:2], scalar1=1.0 / D,
                                                      scalar2=sa[:, 2:3], op0=ALU.mult, op1=ALU.add),
                     reads=[st.bn.buf, st.mv.buf], writes=[st.mv.buf])
            S.op("dve", lambda e: e.scalar_tensor_tensor(out=src.ap, in0=src.ap, scalar=mv[:, 0:1], in1=gt.ap,
                                                         op0=ALU.subtract, op1=ALU.mult),
                 reads=[src.buf, st.mv.buf, gt.buf], writes=[src.buf])
            S.op("act", lambda e: e.activation(out=st.rs.ap[:, 0:1], in_=mv[:, 1:2], func=AF.Sqrt, bias=EPS, scale=1.0),
                 reads=[st.mv.buf], writes=[st.rs.buf])

        def ln_norm(src, dst, st, hb=None, valid_ap=None, bt=None):
            bt = bt or btile
            rs = st.rs.ap
            S.op("dve", lambda e: e.reciprocal(out=rs[:, 1:2], in_=rs[:, 0:1]), reads=[st.rs.buf], writes=[st.rs.buf])
            S.op("dve", lambda e: e.scalar_tensor_tensor(out=dst.ap, in0=src.ap, scalar=rs[:, 1:2], in1=bt.ap,
                                                         op0=ALU.mult, op1=ALU.add),
                 reads=[src.buf, st.rs.buf, bt.buf], writes=[dst.buf])
            if hb is not None:
                S.op("act", lambda e: e.activation(out=hb.ap, in_=dst.ap, func=AF.Identity, scale=valid_ap),
                     reads=[dst.buf, tokvalid.buf], writes=[hb.buf])

        def tr_mm(hb, pb):
            psb = ps[:, pb * 512:(pb + 1) * 512].bitcast(BF16)
            for kc in range(8):
                S.op("pe", lambda e, kc=kc: e.transpose(out=psb[:, kc * 128:(kc + 1) * 128],
                                                        in_=hb.ap[:, kc * 128:(kc + 1) * 128], identity=ident.ap),
                     reads=[hb.buf, ident.buf], writes=[bank[pb]])

        def tr_evac(dstT_ap, dst_buf, col0, pb):
            psb = ps[:, pb * 512:(pb + 1) * 512].bitcast(BF16)
            d3 = dstT_ap.rearrange("p (k f) -> p k f", k=8)[:, :, col0:col0 + 128]
            S.op("act", lambda e: e.activation(out=d3, in_=psb.rearrange("p (k f) -> p k f", k=8), func=AF.Copy),
                 reads=[bank[pb]], writes=[dst_buf])

        class TiledT:
            def __init__(self, name, lo):
                self.ap = b16ap(lo, 8 * TE)
                self.bufs = S.buf_group([f"{name}_{i}" for i in range(NBE)], lo, lo + 8 * TE // 2)

            def cols(self, c0, c1):
                return [self.bufs[i] for i in range(c0 // 128, (c1 - 1) // 128 + 1)]

        def proj_fm(w3, wbuf, src3, srcbufs, c_lo, ncols, r0, nk=8, kparts=128, src_k=None):
            g0 = 0
            while g0 < ncols:
                g1 = min(g0 + 512, ncols)
                for k in range(nk):
                    S.op("pe", lambda e, k=k, g0=g0, g1=g1: e.matmul(
                        out=ps[:, r0 + g0:r0 + g1], lhsT=w3[0:kparts, k, :],
                        rhs=src3[0:kparts, k, c_lo + g0:c_lo + g1], start=(k == 0), stop=(k == nk - 1)),
                        reads=[wbuf] + (srcbufs(c_lo + g0, c_lo + g1) if callable(srcbufs) else srcbufs),
                        writes=banks_of(r0 + g0, r0 + g1), name="proj")
                g0 = g1

        def do_chunk(ch):
            blk0 = ch * NT
            row0 = ch * TC
            hT = TiledT(f"hT{ch}", hT_lo)
            hT3 = hT.ap.rearrange("p (k f) -> p k f", k=8)
            A0 = Alloc(R_lo, R_lo + R_W)
            NX = 4
            xt = [T(f"xt{ch}_{i}", A0.take(1024), 1024, F32) for i in range(NX)]
            hb0 = [T(f"hb0{ch}_{i}", A0.take(512), 1024, BF16) for i in range(2)]
            g0tile = T(f"g0tile{ch}", A0.take(1024), 1024, F32)
            b0tile = T(f"b0tile{ch}", A0.take(1024), 1024, F32)
            junk_ap = ps[:, 4 * 512:6 * 512]
            S.op("sp", lambda e: e.dma_start(out=g0tile.ap, in_=lng_d[0].partition_broadcast(128)),
                 writes=[g0tile.buf], lane=lane_g0, name="ld_g0")
            S.op("sp", lambda e: e.dma_start(out=b0tile.ap, in_=lnb_d[0].partition_broadcast(128)),
                 writes=[b0tile.buf], lane=lane_b0, name="ld_b0")

            p0_order = list(range(NBE)) if ch == 0 else [0, 1, NT + 2, NT + 3] + list(range(2, NT + 2))

            def p0_dst(eb, n):
                return hres[eb - 1] if 1 <= eb <= NT + 2 else xt[n % NX]

            def p0_load(n):
                eb = p0_order[n]
                S.op("sp", lambda e: e.dma_start(out=xt[n % NX].ap, in_=x_d[row0 + eb * 128:row0 + (eb + 1) * 128, :]),
                     writes=[xt[n % NX].buf], lane=lane_x[n % NX], name="ld_x")

            def use_act(n):
                return True if ch > 0 else (n % 2 == 1)

            def p0_act(n):
                if use_act(n):
                    ln_stats_a(xt[n % NX], stat0[n % 3], junk_ap, [bank[4], bank[5]])

            def p0_norm(n):
                eb = p0_order[n]
                ln_norm(xt[n % NX], p0_dst(eb, n), stat0[n % 3], hb=hb0[n % 2],
                        valid_ap=tokvalid.ap[:, blk0 + eb:blk0 + eb + 1], bt=b0tile)

            p0 = Pipe(NBE, [
                p0_load,
                p0_act,
                lambda n: ln_stats(xt[n % NX], stat0[n % 3], from_act=use_act(n), gt=g0tile),
                p0_norm,
                lambda n: tr_mm(hb0[n % 2], 6 + n % 2),
                lambda n: tr_evac(hT.ap, hT.bufs[p0_order[n]], p0_order[n] * 128, 6 + n % 2),
            ])
            yield p0

            region_b(ch)
            brel, esink = cur["brel"], cur["esink"]
            OA = Alloc(OV, AW)
            mT = [T(f"mT{ch}_{f}", OA.take(640), 1280, BF16) for f in range(8)]
            attT = T(f"attT{ch}", OA.take(4 * TQ // 2), 4 * TQ, BF16)
            attT3 = attT.ap.rearrange("p (g q) -> p g q", g=4)
            convT = T(f"convT{ch}", OA.take(4 * TQ // 2), 4 * TQ, BF16)
            convT3 = convT.ap.rearrange("p (j q) -> p j q", j=4)
            SA = OA.take(7184)
            SB = OA.take(4096)
            SC = OA.take(3 * 1032)
            if ch == 0:
                dbg("h_t1", hres[1].ap, [hres[1].buf])
                dbg("hT", hT.ap, hT.bufs, BF16)
            if ch == 0:
                late_consts()
            for f in range(8):
                S.op("dve", lambda e, f=f: e.memset(mT[f].ap[:, 0:QLO - 128], 0.0), writes=[mT[f].buf], name="mT0")
                S.op("dve", lambda e, f=f: e.memset(mT[f].ap[:, QLO - 128 + TQ:1280], 0.0), writes=[mT[f].buf], name="mT0")

            A1 = Alloc(SA, SA + 7184 + 4096)
            kT = T(f"kT{ch}", A1.take(TE // 2), TE, BF16)
            vt = T(f"v{ch}", A1.take(TE // 2), TE, BF16)
            vt3 = vt.ap.rearrange("p (b f) -> p b f", b=NBE)
            qT = T(f"qT{ch}", A1.take(4 * TQ // 2), 4 * TQ, BF16)
            qT3 = qT.ap.rearrange("p (c q) -> p c q", c=4)
            tt = [T(f"tt{ch}_{i}", A1.take(1024), 1024, F32) for i in range(2)]
            PTr = [T(f"PT{ch}_{i}", A1.take(512), 1024, BF16) for i in range(3)]
            dtmp = [T(f"dtmp{ch}_{i}", A1.take(512), 512, F32) for i in range(2)]

            wt, w3 = load_w(win_d[0])
            r0 = region()
            proj_fm(w3, wt.buf, hT3, hT.cols, 0, TE, r0)
            S.op("act", lambda e, r0=r0: e.activation(out=kT.ap, in_=ps[:, r0:r0 + TE], func=AF.Copy),
                 reads=banks_of(r0, r0 + TE), writes=[kT.buf], name="evac_k")
            wt, w3 = load_w(win_d[1])
            r0 = region()
            for eb in range(NBE):
                for k in range(8):
                    S.op("pe", lambda e, k=k, eb=eb, r0=r0, w3=w3: e.matmul(
                        out=ps[:, r0 + eb * 128:r0 + (eb + 1) * 128], lhsT=hT3[:, k, eb * 128:(eb + 1) * 128],
                        rhs=w3[:, k, :], start=(k == 0), stop=(k == 7)),
                        reads=[wt.buf, hT.bufs[eb]], writes=banks_of(r0 + eb * 128, r0 + (eb + 1) * 128), name="vproj")
            S.op("dve", lambda e, r0=r0: e.tensor_copy(out=vt.ap, in_=ps[:, r0:r0 + TE]),
                 reads=banks_of(r0, r0 + TE), writes=[vt.buf], name="evac_v")
            for c in range(4):
                wt, w3 = load_w(win_d[2 + c])
                r0 = region()
                proj_fm(w3, wt.buf, hT3, hT.cols, QLO, TQ, r0)
                eng = "act" if c % 2 == 0 else "dve"
                if eng == "act":
                    S.op("act", lambda e, r0=r0, c=c: e.activation(out=qT3[:, c, :], in_=ps[:, r0:r0 + TQ], func=AF.Copy),
                         reads=banks_of(r0, r0 + TQ), writes=[qT.buf], name="evac_q")
                else:
                    S.op("dve", lambda e, r0=r0, c=c: e.tensor_copy(out=qT3[:, c, :], in_=ps[:, r0:r0 + TQ]),
                         reads=banks_of(r0, r0 + TQ), writes=[qT.buf], name="evac_q")

            if ch == 0:
                dbg("kT", kT.ap, [kT.buf], BF16)
                dbg("vt", vt.ap, [vt.buf], BF16)
                dbg("qT", qT.ap, [qT.buf], BF16)
            qblocks = [(1, 127, 1, 0)] + [(eb, 0, 128, (eb - 2) * 128 + 1) for eb in range(2, NT + 2)] + \
                      [(NT + 2, 0, 1, TQ - 1)]
            brel4 = brel.ap.rearrange("p (j h q) -> p j h q", j=3, h=8)
            esink3 = esink.ap.rearrange("p (g q) -> p g q", g=4)
            ttc = {"i": 0}

            def at_qk(m):
                b, j = m // 3, m % 3
                eb, ql0, qn, qc0 = qblocks[b]
                n4 = 4 * qn
                kb = eb - 1 + j
                pr = m % 2
                for kvh in range(2):
                    p0 = kvh * 64
                    bk = 2 * pr + kvh
                    S.op("pe", lambda e, bk=bk, p0=p0: e.matmul(
                        out=ps[:, bk * 512:bk * 512 + n4].rearrange("p (g q) -> p g q", g=4),
                        lhsT=kT.ap[p0:p0 + 64, kb * 128:(kb + 1) * 128],
                        rhs=qT3[p0:p0 + 64, :, qc0:qc0 + qn], start=True, stop=True),
                        reads=[kT.buf, qT.buf], writes=[bank[bk]], name="qk")

            def at_exp(m):
                b, j = m // 3, m % 3
                eb, ql0, qn, qc0 = qblocks[b]
                n4 = 4 * qn
                kb = eb - 1 + j
                pr = m % 2
                tb = tt[m % 2]
                pt = PTr[m % 3]
                S.op("dve", lambda e: e.scalar_tensor_tensor(
                    out=tb.ap[:, 0:2 * n4].rearrange("p (k g q) -> p k g q", k=2, g=4),
                    in0=ps[:, 2 * pr * 512:(2 * pr + 2) * 512].rearrange("p (k c) -> p k c", k=2)[:, :, 0:n4]
                    .rearrange("p k (g q) -> p k g q", g=4),
                    scalar=0.125,
                    in1=brel4[:, j, :, ql0:ql0 + qn].rearrange("p (k g) q -> p k g q", k=2),
                    op0=ALU.mult, op1=ALU.add),
                    reads=[bank[2 * pr], bank[2 * pr + 1], brel.buf], writes=[tb.buf], name="sbias")
                S.op("act", lambda e: e.activation(
                    out=pt.ap[:, 0:2 * n4], in_=tb.ap[:, 0:2 * n4], func=AF.Exp,
                    bias=kbias.ap[:, blk0 + kb:blk0 + kb + 1], scale=1.0),
                    reads=[tb.buf, kbias.buf], writes=[pt.buf], name="exp")

            def at_pv(m):
                b, j = m // 3, m % 3
                eb, ql0, qn, qc0 = qblocks[b]
                i = b % 2
                n4 = 4 * qn
                kb = eb - 1 + j
                pt = PTr[m % 3]
                bo, bd = 4 + 2 * i, 5 + 2 * i
                for kvh in range(2):
                    p0 = kvh * 64
                    S.op("pe", lambda e, p0=p0, kvh=kvh: e.matmul(
                        out=ps[p0:p0 + 64, bo * 512:bo * 512 + n4], lhsT=vt3[:, kb, p0:p0 + 64],
                        rhs=pt.ap[:, kvh * n4:(kvh + 1) * n4], start=(j == 0), stop=(j == 2)),
                        reads=[vt.buf, pt.buf], writes=[bank[bo]], name="pv")
                for kvh in range(2):
                    p0 = kvh * 64
                    S.op("pe", lambda e, p0=p0, kvh=kvh: e.matmul(
                        out=ps[p0:p0 + 64, bd * 512:bd * 512 + n4], lhsT=ones.ap,
                        rhs=pt.ap[:, kvh * n4:(kvh + 1) * n4], start=(j == 0), stop=(j == 2)),
                        reads=[ones.buf, pt.buf], writes=[bank[bd]], name="den")

            def at_norm(m):
                b, j = m // 3, m % 3
                if j != 2:
                    return
                eb, ql0, qn, qc0 = qblocks[b]
                i = b % 2
                n4 = 4 * qn
                bo, bd = 4 + 2 * i, 5 + 2 * i
                S.op("dve", lambda e: e.tensor_tensor(
                    out=dtmp[i].ap[:, 0:n4].rearrange("p (g q) -> p g q", g=4),
                    in0=ps[:, bd * 512:bd * 512 + n4].rearrange("p (g q) -> p g q", g=4),
                    in1=esink3[:, :, ql0:ql0 + qn], op=ALU.add),
                    reads=[bank[bd], esink.buf], writes=[dtmp[i].buf], name="den_sink")
                S.op("act", lambda e: e.activation(out=dtmp[i].ap[:, 0:n4], in_=dtmp[i].ap[:, 0:n4], func=AF.Ln),
                     reads=[dtmp[i].buf], writes=[dtmp[i].buf], name="ln_den")
                S.op("act", lambda e: e.activation(out=dtmp[i].ap[:, 0:n4], in_=dtmp[i].ap[:, 0:n4], func=AF.Exp, scale=-1.0),
                     reads=[dtmp[i].buf], writes=[dtmp[i].buf], name="recip_den")

            def at_norm2(m):
                b, j = m // 3, m % 3
                if j != 2:
                    return
                eb, ql0, qn, qc0 = qblocks[b]
                i = b % 2
                n4 = 4 * qn
                bo, bd = 4 + 2 * i, 5 + 2 * i
                S.op("dve", lambda e: e.tensor_tensor(
                    out=attT3[:, :, qc0:qc0 + qn],
                    in0=ps[:, bo * 512:bo * 512 + n4].rearrange("p (g q) -> p g q", g=4),
                    in1=dtmp[i].ap[:, 0:n4].rearrange("p (g q) -> p g q", g=4), op=ALU.mult),
                    reads=[bank[bo], dtmp[i].buf], writes=[attT.buf], name="att_norm")

            pipeline(3 * len(qblocks), [at_qk, at_exp, at_pv, at_norm, at_norm2])

            wo = T(f"wo{ch}", SB, 8 * 1024, BF16)
            wo3 = wo.ap.rearrange("p (k f) -> p k f", k=8)
            S.op("pool", lambda e: e.dma_start(out=wo3, in_=wo_d, max_dma_last_dim=8192), writes=[wo.buf],
                 lane=lane_wo, name="ld_wo")

            if ch == 0:
                dbg("attT", attT.ap, [attT.buf], BF16)
            ccs = T(f"ccs{ch}", SC, TP, F32)
            pp = T(f"pp{ch}", SC + 1032, TP, F32)
            acc = T(f"acc{ch}", SC + 2064, TQ, F32)
            for j in range(4):
                wt, w3 = load_w(win_d[6 + 3 * j + 1])
                rc = region()
                proj_fm(w3, wt.buf, hT3, hT.cols, QLO - 1, TP, rc)
                S.op("act", lambda e, rc=rc: e.activation(out=ccs.ap, in_=ps[:, rc:rc + TP], func=AF.Copy),
                     reads=banks_of(rc, rc + TP), writes=[ccs.buf], name="evac_cc")
                wt, w3 = load_w(win_d[6 + 3 * j + 2])
                rx = region()
                proj_fm(w3, wt.buf, hT3, hT.cols, QLO - 1, TP, rx)
                S.op("dve", lambda e, rx=rx: e.tensor_tensor(out=pp.ap, in0=ps[:, rx:rx + TP], in1=ccs.ap, op=ALU.mult),
                     reads=banks_of(rx, rx + TP) + [ccs.buf], writes=[pp.buf], name="p_mul")
                wt, w3 = load_w(win_d[6 + 3 * j + 0])
                rb = rx
                proj_fm(w3, wt.buf, hT3, hT.cols, QLO, TQ, rb)
                S.op("act", lambda e, j=j: e.activation(out=acc.ap, in_=pp.ap[:, 1:1 + TQ], func=AF.Identity,
                                                        bias=cbs.ap[:, j:j + 1], scale=cw.ap[:, 3 * j + 1:3 * j + 2]),
                     reads=[pp.buf, cbs.buf, cw.buf], writes=[acc.buf], name="conv1")
                S.op("dve", lambda e, j=j: e.scalar_tensor_tensor(out=acc.ap, in0=pp.ap[:, 0:TQ],
                                                                  scalar=cw.ap[:, 3 * j:3 * j + 1], in1=acc.ap,
                                                                  op0=ALU.mult, op1=ALU.add),
                     reads=[pp.buf, cw.buf, acc.buf], writes=[acc.buf], name="conv0")
                S.op("dve", lambda e, j=j: e.scalar_tensor_tensor(out=acc.ap, in0=pp.ap[:, 2:2 + TQ],
                                                                  scalar=cw.ap[:, 3 * j + 2:3 * j + 3], in1=acc.ap,
                                                                  op0=ALU.mult, op1=ALU.add),
                     reads=[pp.buf, cw.buf, acc.buf], writes=[acc.buf], name="conv2")
                S.op("dve", lambda e, j=j, rb=rb: e.tensor_tensor(out=convT3[:, j, :], in0=ps[:, rb:rb + TQ],
                                                                  in1=acc.ap, op=ALU.mult),
                     reads=banks_of(rb, rb + TQ) + [acc.buf], writes=[convT.buf], name="conv_gate")

            if ch == 0:
                dbg("convT", convT.ap, [convT.buf], BF16)
            A2 = Alloc(SA, SA + 7184)
            gab = [T(f"ga{ch}_{i}", A2.take(TQ), TQ, F32) for i in range(2)]
            gcb = [T(f"gc{ch}_{i}", A2.take(TQ), TQ, F32) for i in range(2)]
            t1b = [T(f"t1{ch}_{i}", A2.take(TQ), TQ, F32) for i in range(2)]
            for f in range(8):
                i = f % 2
                wt, w3 = load_w(win_d[18 + 2 * f])
                ra = region()
                proj_fm(w3, wt.buf, hT3, hT.cols, QLO, TQ, ra)
                S.op("act", lambda e, ra=ra, f=f, i=i: e.activation(out=gab[i].ap, in_=ps[:, ra:ra + TQ], func=AF.Sigmoid,
                                                                    bias=bg.ap[:, f:f + 1], scale=1.0),
                     reads=banks_of(ra, ra + TQ) + [bg.buf], writes=[gab[i].buf], name="sig_a")
                wt, w3 = load_w(win_d[18 + 2 * f + 1])
                rc = region()
                proj_fm(w3, wt.buf, hT3, hT.cols, QLO, TQ, rc)
                S.op("act", lambda e, rc=rc, f=f, i=i: e.activation(out=gcb[i].ap, in_=ps[:, rc:rc + TQ], func=AF.Sigmoid,
                                                                    bias=bg.ap[:, 8 + f:9 + f], scale=1.0),
                     reads=banks_of(rc, rc + TQ) + [bg.buf], writes=[gcb[i].buf], name="sig_c")
                wt, w3 = load_w(wa_d[f], nk=4)
                ra = region()
                proj_fm(w3, wt.buf, attT3, [attT.buf], 0, TQ, ra, nk=4)
                S.op("dve", lambda e, ra=ra, i=i: e.tensor_tensor(out=t1b[i].ap, in0=ps[:, ra:ra + TQ], in1=gab[i].ap,
                                                                  op=ALU.mult),
                     reads=banks_of(ra, ra + TQ) + [gab[i].buf], writes=[t1b[i].buf], name="gate_a")
                wt, w3 = load_w(wc_d[f], nk=4)
                rc = region()
                proj_fm(w3, wt.buf, convT3, [convT.buf], 0, TQ, rc, nk=4)
                S.op("dve", lambda e, rc=rc, i=i: e.tensor_tensor(out=gcb[i].ap, in0=ps[:, rc:rc + TQ], in1=gcb[i].ap,
                                                                  op=ALU.mult),
                     reads=banks_of(rc, rc + TQ) + [gcb[i].buf], writes=[gcb[i].buf], name="gate_c")
                S.op("dve", lambda e, f=f, i=i: e.tensor_tensor(out=mT[f].ap[:, QLO - 128:QLO - 128 + TQ], in0=t1b[i].ap,
                                                                 in1=gcb[i].ap, op=ALU.add),
                     reads=[t1b[i].buf, gcb[i].buf], writes=[mT[f].buf], name="merge")

            if ch == 0:
                dbg("mT0", mT[0].ap, [mT[0].buf], BF16)
                dbg("mT7", mT[7].ap, [mT[7].buf], BF16)
            A3 = Alloc(SA, SA + 7184)
            tmp2 = [T(f"tmp2{ch}_{i}", A3.take(1024), 1024, F32) for i in range(4)]
            hb2 = [T(f"hb2{ch}_{i}", A3.take(512), 1024, BF16) for i in range(2)]
            junk2 = T(f"junk2{ch}", A3.take(512), 1024, BF16)
            h2T = TiledT(f"h2T{ch}", hT_lo)
            h2T3 = h2T.ap.rearrange("p (k f) -> p k f", k=8)
            ln_load_gb(1)

            p2_order = [0, NT + 1] + list(range(1, NT + 1))

            def p2_mm(n):
                t = p2_order[n]
                r0 = (n % 2) * 1024
                for half in range(2):
                    for k in range(8):
                        S.op("pe", lambda e, k=k, half=half: e.matmul(
                            out=ps[:, r0 + half * 512:r0 + (half + 1) * 512], lhsT=mT[k].ap[:, t * 128:(t + 1) * 128],
                            rhs=wo3[:, k, half * 512:(half + 1) * 512], start=(k == 0), stop=(k == 7)),
                            reads=[mT[k].buf, wo.buf], writes=[bank[r0 // 512 + half]], name="wo_mm")

            def p2_resid(n):
                t = p2_order[n]
                r0 = (n % 2) * 1024
                i = n % 4
                S.op("dve", lambda e: e.scalar_tensor_tensor(
                    out=tmp2[i].ap, in0=hres[t].ap, scalar=ALPHA, in1=ps[:, r0:r0 + 1024], op0=ALU.mult, op1=ALU.add),
                    reads=[hres[t].buf] + banks_of(r0, r0 + 1024), writes=[tmp2[i].buf], name="resid1")

            def p2_norm(n):
                t = p2_order[n]
                ln_norm(tmp2[n % 4], hres[t], stat[n % 3], hb=hb2[n % 2],
                        valid_ap=tokvalid.ap[:, blk0 + t + 1:blk0 + t + 2])

            pipeline(NT + 2, [
                p2_mm,
                p2_resid,
                lambda n: ln_stats_a(tmp2[n % 4], stat[n % 3], ps[:, 4 * 512:6 * 512], [bank[4], bank[5]]),
                lambda n: ln_stats(tmp2[n % 4], stat[n % 3], from_act=True),
                p2_norm,
                lambda n: tr_mm(hb2[n % 2], 6 + n % 2),
                lambda n: tr_evac(h2T.ap, h2T.bufs[p2_order[n] + 1], (p2_order[n] + 1) * 128, 6 + n % 2),
            ])

            if ch == 0:
                dbg("h2_t1", hres[1].ap, [hres[1].buf])
                dbg("h2T", h2T.ap, h2T.bufs[1:NT + 3], BF16)
            OB = Alloc(OV, AW)
            gT = [T(f"gT{ch}_{j}", OB.take(512), 1024, BF16) for j in range(NJ)]
            wd_lo = OB.take(NJ * 512)
            accs = [[T(f"facc{ch}_{i}_{s}", OB.take(1024), 1024, F32) for s in range(2)] for i in range(2)]
            o_lo = accs[0][0].buf.lo
            wdt = T(f"wd{ch}", wd_lo, NJ * 1024, BF16)
            wd3 = wdt.ap.rearrange("p (k f) -> p k f", k=NJ)
            for j in range(NJ):
                i = j % 2
                S.op("pool", lambda e, j=j: e.dma_start(out=wd3[:, j, :], in_=wd_d[:, j, :], max_dma_last_dim=8192),
                     writes=[wdt.buf], lane=lane_wd, name="ld_wd")
                for s in range(2):
                    ci = 2 * j + s
                    wt, w3 = load_w(wup_d[ci])
                    r0 = (ci % 3) * 1024
                    hbk = 6 + ci % 2
                    h0 = hbk * 512 + 2 * (ci // 2 % 64)
                    proj_fm(w3, wt.buf, h2T3, h2T.cols, QLO + 1, TC, r0)
                    for k in range(8):
                        S.op("pe", lambda e, k=k, w3=w3, h0=h0: e.matmul(
                            out=ps[:, h0:h0 + 2], lhsT=w3[:, k, :], rhs=h2T3[:, k, QLO:QLO + TQ:TQ - 1],
                            start=(k == 0), stop=(k == 7)),
                            reads=[wt.buf, h2T.bufs[1], h2T.bufs[NT + 2]], writes=[bank[hbk]], name="proj_halo")
                    a = accs[i][s]
                    rb = banks_of(r0, r0 + TC)
                    S.op("act", lambda e, r0=r0, a=a, ci=ci: e.activation(
                        out=a.ap, in_=ps[:, r0:r0 + TC], func=AF.Identity, bias=fcb.ap[:, ci:ci + 1],
                        scale=fcw.ap[:, 3 * ci + 1:3 * ci + 2]),
                        reads=rb + [fcb.buf, fcw.buf], writes=[a.buf], name="fconv1")
                    S.op("act", lambda e, a=a, ci=ci, h0=h0: e.activation(
                        out=a.ap[:, 0:1], in_=ps[:, h0:h0 + 1], func=AF.Identity, bias=a.ap[:, 0:1],
                        scale=fcw.ap[:, 3 * ci:3 * ci + 1]),
                        reads=[bank[hbk], fcw.buf, a.buf], writes=[a.buf], name="fedgeL")
                    S.op("act", lambda e, a=a, ci=ci, h0=h0: e.activation(
                        out=a.ap[:, TC - 1:TC], in_=ps[:, h0 + 1:h0 + 2], func=AF.Identity, bias=a.ap[:, TC - 1:TC],
                        scale=fcw.ap[:, 3 * ci + 2:3 * ci + 3]),
                        reads=[bank[hbk], fcw.buf, a.buf], writes=[a.buf], name="fedgeR")
                    S.op("dve", lambda e, r0=r0, a=a, ci=ci: e.scalar_tensor_tensor(
                        out=a.ap[:, 1:TC], in0=ps[:, r0:r0 + TC - 1], scalar=fcw.ap[:, 3 * ci:3 * ci + 1],
                        in1=a.ap[:, 1:TC], op0=ALU.mult, op1=ALU.add),
                        reads=rb + [fcw.buf, a.buf], writes=[a.buf], name="fconv0")
                    S.op("dve", lambda e, r0=r0, a=a, ci=ci: e.scalar_tensor_tensor(
                        out=a.ap[:, 0:TC - 1], in0=ps[:, r0 + 1:r0 + TC], scalar=fcw.ap[:, 3 * ci + 2:3 * ci + 3],
                        in1=a.ap[:, 0:TC - 1], op0=ALU.mult, op1=ALU.add),
                        reads=rb + [fcw.buf, a.buf], writes=[a.buf], name="fconv2")
                aa, uu = accs[i][0], accs[i][1]
                S.op("act", lambda e, aa=aa: e.activation(out=aa.ap, in_=aa.ap, func=AF.Silu),
                     reads=[aa.buf], writes=[aa.buf], name="silu")
                S.op("dve", lambda e, aa=aa, uu=uu, j=j: e.tensor_tensor(out=gT[j].ap, in0=aa.ap, in1=uu.ap, op=ALU.mult),
                     reads=[aa.buf, uu.buf], writes=[gT[j].buf], name="glu")
            if ch == 0:
                dbg("gT0", gT[0].ap, [gT[0].buf], BF16)
                dbg("gT21", gT[21].ap, [gT[21].buf], BF16)
            OT = Alloc(o_lo, AW)
            tmp4 = [T(f"tmp4{ch}_{i}", OT.take(1024), 1024, F32) for i in range(2)]
            ot = [T(f"ot{ch}_{i}", OT.take(1024), 1024, F32) for i in range(2)]
            ln_load_gb(2)
            out_ops = []

            def p4_mm(t):
                r0 = (t % 2) * 1024
                for half in range(2):
                    for k in range(NJ):
                        S.op("pe", lambda e, k=k, half=half: e.matmul(
                            out=ps[:, r0 + half * 512:r0 + (half + 1) * 512], lhsT=gT[k].ap[:, t * 128:(t + 1) * 128],
                            rhs=wd3[:, k, half * 512:(half + 1) * 512], start=(k == 0), stop=(k == NJ - 1)),
                            reads=[gT[k].buf, wdt.buf], writes=[bank[r0 // 512 + half]], name="wd_mm")

            act4 = (ch < NCH - 1)
            t4 = tmp4 + ot

            def p4_resid(t):
                r0 = (t % 2) * 1024
                src = t4[t % 4] if act4 else tmp4[t % 2]
                S.op("dve", lambda e: e.scalar_tensor_tensor(
                    out=src.ap, in0=hres[t + 1].ap, scalar=ALPHA, in1=ps[:, r0:r0 + 1024], op0=ALU.mult, op1=ALU.add),
                    reads=[hres[t + 1].buf] + banks_of(r0, r0 + 1024), writes=[src.buf], name="resid2")

            def p4_out(t, src):
                orow = ch * TC + t * 128
                o = S.op("sp", lambda e: e.dma_start(out=out_d[orow:orow + 128, :], in_=src.ap),
                         reads=[src.buf], lane=lane_o[t % 4 if act4 else t % 2], name="st_out")
                out_ops.append(o)

            if act4:
                junk4 = ps[:, 4 * 512:6 * 512]

                def p4_norm(t):
                    ln_norm(t4[t % 4], t4[t % 4], stat[t % 3])
                    p4_out(t, t4[t % 4])

                yield Pipe(NT, [
                    p4_mm,
                    p4_resid,
                    lambda t: ln_stats_a(t4[t % 4], stat[t % 3], junk4, [bank[4], bank[5]]),
                    lambda t: ln_stats(t4[t % 4], stat[t % 3], from_act=True),
                    p4_norm,
                ])
            else:
                def p4_stats(t):
                    p4_resid(t)
                    ln_stats(tmp4[t % 2], stat[t % 3])

                def p4_norm(t):
                    i = t % 2
                    ln_norm(tmp4[i], ot[i], stat[t % 3])
                    p4_out(t, ot[i])

                yield Pipe(NT, [p4_mm, p4_stats, p4_norm])
            out_ops_all.extend(out_ops)
        out_ops_all = []
        g0 = do_chunk(0)
        next(g0).run()
        p4 = next(g0)
        g1 = do_chunk(1)
        p0n = next(g1)
        j = 0
        for sidx in range(p4.nsteps):
            p4.step(sidx)
            for _ in range(2):
                if j < p0n.nsteps and j <= sidx + 6:
                    p0n.step(j)
                    j += 1
        while j < p0n.nsteps:
            p0n.step(j)
            j += 1
        for _ in g0:
            pass
        p4b = next(g1)
        p4b.run()
        for _ in g1:
            pass
        last_per_lane = {}
        for o in out_ops_all:
            last_per_lane[id(o.lane)] = o
        final = list(last_per_lane.values())

        S.emit(nc, es, final + dbg_list)
    return nc


def _rel_bucket(rel):
    half = 16
    max_exact = 8
    offset = np.where(rel > 0, half, 0)
    n = np.abs(rel)
    nf = np.maximum(n, 1).astype(np.float32)
    large = max_exact + (np.log(nf / max_exact) / math.log(128 / max_exact) * (half - max_exact)).astype(np.int32)
    large = np.minimum(large, half - 1)
    return offset + np.where(n < max_exact, n, large)


def _tile_w(w, col_idx):
    sub = w[:, col_idx]
    K = sub.shape[0]
    return np.ascontiguousarray(sub.reshape(K // 128, 128, -1).transpose(1, 0, 2))


_PROGRAM = None


def kernel(x, ln_in_g, ln_in_b, w_in, b_gates, attn_sink, rel_bias, conv_w, conv_b,
           w_att_branch, w_conv_branch, w_o, ln_mix_g, ln_mix_b,
           w_ffn_up, ffn_conv_w, ffn_conv_b, w_ffn_down, ln_ffn_g, ln_ffn_b):
    global _PROGRAM
    f32 = np.float32
    x = np.asarray(x, f32)
    w_in0 = np.asarray(w_in, f32)[0]
    ar = np.arange(128)
    cols = []
    cols.append(512 + ar)
    cols.append(640 + ar)
    for c in range(4):
        cols.append(np.concatenate([c * 64 + np.arange(64), (4 + c) * 64 + np.arange(64)]))
    for j in range(4):
        cols.append(768 + j * 128 + ar)
        cols.append(1280 + j * 128 + ar)
        cols.append(1792 + j * 128 + ar)
    for f in range(8):
        cols.append(2304 + f * 128 + ar)
        cols.append(3328 + f * 128 + ar)
    win = np.stack([_tile_w(w_in0, c) for c in cols])
    bgv = np.asarray(b_gates, f32)[0]
    bg = np.ascontiguousarray(bgv.reshape(16, 128).T)
    cwv = np.asarray(conv_w, f32)[0]
    cw = np.ascontiguousarray(cwv.reshape(3, 4, 128).transpose(2, 1, 0).reshape(128, 12))
    cbias = np.ascontiguousarray(np.asarray(conv_b, f32)[0].reshape(4, 128).T)
    wab = np.asarray(w_att_branch, f32)[0]
    wa = np.ascontiguousarray(wab.reshape(2, 4, 64, 8, 128).transpose(3, 0, 2, 1, 4).reshape(8, 128, 4, 128))
    wcb = np.asarray(w_conv_branch, f32)[0]
    wc = np.ascontiguousarray(wcb.reshape(4, 128, 8, 128).transpose(2, 1, 0, 3))
    wo = np.ascontiguousarray(np.asarray(w_o, f32)[0].reshape(8, 128, 1024).transpose(1, 0, 2))
    wup0 = np.asarray(w_ffn_up, f32)[0]
    upcols = []
    for j in range(NJ):
        upcols.append(j * 128 + ar)
        upcols.append(DFF + j * 128 + ar)
    wup = np.stack([_tile_w(wup0, c) for c in upcols])
    fw = np.asarray(ffn_conv_w, f32)[0]
    fb = np.asarray(ffn_conv_b, f32)[0]
    fcw = np.ascontiguousarray(np.stack([fw[:, c] for c in upcols]).transpose(2, 0, 1).reshape(128, 2 * NJ * 3))
    fcb = np.ascontiguousarray(np.stack([fb[c] for c in upcols]).T)
    wd = np.ascontiguousarray(np.asarray(w_ffn_down, f32)[0].reshape(NJ, 128, 1024).transpose(1, 0, 2))
    c_i = np.arange(128)[:, None]
    q_i = np.arange(128)[None, :]
    rb = np.asarray(rel_bias, f32)
    brel = np.empty((128, 3, 8, 128), f32)
    for j in range(3):
        rel = (j - 1) * 128 + c_i - q_i
        bt = rb[_rel_bucket(rel)]
        bt = np.where((np.abs(rel) <= 128)[:, :, None], bt, f32(NEG))
        brel[:, j] = bt.transpose(0, 2, 1)
    brel = np.ascontiguousarray(brel.reshape(128, 3 * 8 * 128))
    sk = np.asarray(attn_sink, f32)[0].reshape(2, 1, 4, 1)
    sinkx = np.ascontiguousarray(np.broadcast_to(sk, (2, 64, 4, 128)).reshape(128, 512))

    shared = {
        "win": win, "bg": bg, "cw": cw, "cbias": cbias, "wa": wa, "wc": wc, "wo": wo,
        "ln_in_g": np.asarray(ln_in_g, f32).reshape(1, D), "ln_in_b": np.asarray(ln_in_b, f32).reshape(1, D),
        "ln_mix_g": np.asarray(ln_mix_g, f32).reshape(1, D), "ln_mix_b": np.asarray(ln_mix_b, f32).reshape(1, D),
        "ln_ffn_g": np.asarray(ln_ffn_g, f32).reshape(1, D), "ln_ffn_b": np.asarray(ln_ffn_b, f32).reshape(1, D),
        "wup": wup, "fcw": fcw, "fcb": fcb, "wd": wd, "brel": brel, "sinkx": sinkx,
    }
    in_maps = []
    for c in range(NCORES):
        b, half = c // 2, c % 2
        s = half * TOK_CORE
        xe = np.zeros((XROWS, D), f32)
        lo, hi = s - 256, s + TOK_CORE + 256
        slo, shi = max(lo, 0), min(hi, SEQ)
        xe[slo - lo:shi - lo] = x[b, slo:shi]
        tok = lo + np.arange(XROWS)
        valid = ((tok >= 0) & (tok < SEQ)).astype(f32).reshape(NBLK_CORE, 128).T
        m = dict(shared)
        m["x"] = xe
        m["tokvalid"] = np.ascontiguousarray(valid)
        m["kbias"] = np.ascontiguousarray(np.where(valid > 0, f32(0.0), f32(NEG)).astype(f32))
        in_maps.append(m)

    if _PROGRAM is None:
        _PROGRAM = build_program()
    res = run_bass_kernel_spmd(_PROGRAM, in_maps, core_ids=list(range(NCORES)))
    if DEBUG:
        global _LAST
        _LAST = res.results
    out = np.empty((4, SEQ, D), f32)
    for c in range(NCORES):
        b, half = c // 2, c % 2
        out[b, half * TOK_CORE:(half + 1) * TOK_CORE] = res.results[c]["out"]
    return out
```

```python
import math
from contextlib import ExitStack

import numpy as np
import concourse.bass as bass
import concourse.mybir as mybir
from concourse.bass_utils import run_bass_kernel_spmd

F32 = mybir.dt.float32
BF16 = mybir.dt.bfloat16
AF = mybir.ActivationFunctionType
ALU = mybir.AluOpType

D = 1024
SEQ = 4096
NCORES = 8
TOK_CORE = 2048
NCH = 2
TC = TOK_CORE // NCH
NT = TC // 128
NBE = NT + 4
TE = NBE * 128
TQ = TC + 2
TP = TC + 4
QLO = 255
XROWS = TOK_CORE + 512
NBLK_CORE = XROWS // 128
DFF = 2816
NJ = DFF // 128
ALPHA = 2.0 ** 0.25
EPS = 1e-5
NEG = -30000.0
NSLOT = 6
DEBUG = False
N_INCH = 34

ENGS = ("pe", "act", "dve", "pool", "sp")


class Lane:
    def __init__(self, sem):
        self.sem = sem
        self.count = 0


class Op:
    __slots__ = ("eng", "fn", "deps", "marked", "val", "lane", "name")

    def __init__(self, eng, fn, lane=None, name=""):
        self.eng = eng
        self.fn = fn
        self.deps = set()
        self.marked = False
        self.val = None
        self.lane = lane
        self.name = name


class Buf:
    def __init__(self, name, lo=None, hi=None):
        self.name = name
        self.last_w = None
        self.readers = []
        self.lo = lo
        self.hi = hi
        self.dead = False
        self.strict = False


class Sched:
    def __init__(self):
        self.ops = {e: [] for e in ENGS}
        self.all_bufs = []

    def buf(self, name, lo=None, hi=None):
        b = Buf(name, lo, hi)
        if lo is not None:
            for o in self.all_bufs:
                if o.lo is not None and o.lo < hi and lo < o.hi:
                    if o.last_w is not None:
                        b.readers.append(o.last_w)
                    b.readers.extend(o.readers)
                    o.dead = True
            self.all_bufs.append(b)
        return b

    def buf_group(self, names, lo, hi):
        inherited = []
        for o in self.all_bufs:
            if o.lo is not None and o.lo < hi and lo < o.hi:
                if o.last_w is not None:
                    inherited.append(o.last_w)
                inherited.extend(o.readers)
                o.dead = True
        out = []
        for n in names:
            b = Buf(n, lo, hi)
            b.readers = list(inherited)
            out.append(b)
        self.all_bufs.extend(out)
        return out

    def op(self, eng, fn, reads=(), writes=(), lane=None, name=""):
        op = Op(eng, fn, lane, name)
        is_dma = lane is not None

        def same(o):
            return (not is_dma) and (o.lane is None) and o.eng == eng

        for b in reads:
            assert not b.dead, f"read of dead buf {b.name} in {name}"
            w = b.last_w
            if w is not None and not (same(w) and eng == "pe"):
                op.deps.add(w)
        for b in writes:
            assert not b.dead, f"write of dead buf {b.name} in {name}"
            w = b.last_w
            if w is not None and (not same(w) or eng != "pe"):
                op.deps.add(w)
            for r in b.readers:
                if not same(r) or eng != "pe":
                    op.deps.add(r)
        for b in reads:
            b.readers.append(op)
        for b in writes:
            b.last_w = op
            b.readers = []
        op.deps.discard(op)
        self.ops[eng].append(op)
        return op

    def emit(self, nc, es, final_waits):
        for e in ENGS:
            for op in self.ops[e]:
                for d in op.deps:
                    d.marked = True
        for d in final_waits:
            d.marked = True
        esem = {e: es.enter_context(nc.semaphore("eng_" + e)) for e in ENGS}
        cnt = {e: 0 for e in ENGS}
        for e in ENGS:
            for op in self.ops[e]:
                if op.lane is not None:
                    op.lane.count += 16
                    op.val = (op.lane.sem, op.lane.count)
                elif op.marked:
                    cnt[e] += 1
                    op.val = (esem[e], cnt[e])
        block = es.enter_context(nc.Block())

        def run(e, eng):
            seen = {}
            for op in self.ops[e]:
                need = {}
                for d in op.deps:
                    sem, v = d.val
                    k = id(sem)
                    if seen.get(k, (None, 0))[1] >= v:
                        continue
                    if k not in need or need[k][1] < v:
                        need[k] = (sem, v)
                for k, (sem, v) in need.items():
                    eng.wait_ge(sem, v)
                    seen[k] = (sem, v)
                ins = op.fn(eng)
                if op.lane is not None:
                    ins.then_inc(op.lane.sem, 16)
                elif op.marked:
                    ins.then_inc(esem[e], 1)
            if e == "sp":
                need = {}
                for d in final_waits:
                    sem, v = d.val
                    k = id(sem)
                    if k not in need or need[k][1] < v:
                        need[k] = (sem, v)
                for k, (sem, v) in need.items():
                    eng.wait_ge(sem, v)

        block.tensor(lambda eng: run("pe", eng))
        block.scalar(lambda eng: run("act", eng))
        block.vector(lambda eng: run("dve", eng))
        block.gpsimd(lambda eng: run("pool", eng))
        block.sync(lambda eng: run("sp", eng))


def build_program():
    nc = bass.Bass("TRN2", target_bir_lowering=False)

    def din(name, shape):
        return nc.dram_tensor(name, list(shape), F32, kind="ExternalInput").ap()

    x_d = din("x", [XROWS, D])
    tokvalid_d = din("tokvalid", [128, NBLK_CORE])
    kbias_d = din("kbias", [128, NBLK_CORE])
    brel_d = din("brel", [128, 3 * 8 * 128])
    sinkx_d = din("sinkx", [128, 512])
    win_d = din("win", [N_INCH, 128, 8, 128])
    bg_d = din("bg", [128, 16])
    cw_d = din("cw", [128, 12])
    cb_d = din("cbias", [128, 4])
    wa_d = din("wa", [8, 128, 4, 128])
    wc_d = din("wc", [8, 128, 4, 128])
    wo_d = din("wo", [128, 8, 1024])
    lng_d = [din("ln_in_g", [1, D]), din("ln_mix_g", [1, D]), din("ln_ffn_g", [1, D])]
    lnb_d = [din("ln_in_b", [1, D]), din("ln_mix_b", [1, D]), din("ln_ffn_b", [1, D])]
    wup_d = din("wup", [2 * NJ, 128, 8, 128])
    fcw_d = din("fcw", [128, 2 * NJ * 3])
    fcb_d = din("fcb", [128, 2 * NJ])
    wd_d = din("wd", [128, NJ, 1024])
    out_d = nc.dram_tensor("out", [TOK_CORE, D], F32, kind="ExternalOutput").ap()

    S = Sched()
    es = ExitStack()
    dbg_list = []

    def dbg(name, ap, bufs, dt=F32):
        if not DEBUG:
            return
        shape = list(ap.shape)
        dd = nc.dram_tensor("dbg_" + name, shape, dt, kind="ExternalOutput").ap()
        ln = Lane(es.enter_context(nc.semaphore("l_dbg_" + name)))
        o = S.op("sp", lambda e: e.dma_start(out=dd, in_=ap), reads=bufs, lane=ln, name="dbg_" + name)
        dbg_list.append(o)
    with es:
        AW = 53200
        arena = es.enter_context(nc.sbuf_tensor("arena", [128, AW], F32))
        ps = es.enter_context(nc.psum_tensor("ps", [128, 4096], F32))
        es.enter_context(nc.allow_low_precision(reason="bf16 matmul operands, fp32 accumulation"))

        def new_lane(name):
            return Lane(es.enter_context(nc.semaphore(name)))

        class Alloc:
            def __init__(self, lo, hi):
                self.p = lo
                self.hi = hi

            def take(self, words):
                words = (words + 7) // 8 * 8
                lo = self.p
                self.p += words
                assert self.p <= self.hi, (self.p, self.hi)
                return lo

        def f32ap(lo, n, parts=128):
            return arena[0:parts, lo:lo + n]

        def b16ap(lo, n, parts=128):
            assert n % 2 == 0
            return arena[0:parts, lo:lo + n // 2].bitcast(BF16)

        class T:
            def __init__(self, name, lo, n, dt, parts=128):
                words = n if dt is F32 else n // 2
                self.buf = S.buf(name, lo, lo + words)
                self.ap = f32ap(lo, n, parts) if dt is F32 else b16ap(lo, n, parts)

        PA = Alloc(0, AW)
        hres = [T(f"hres{t}", PA.take(1024), 1024, F32) for t in range(NT + 2)]
        hT_lo = PA.take(8 * TE // 2)
        gtile = T("gtile", PA.take(1024), 1024, F32)
        btile = T("btile", PA.take(1024), 1024, F32)
        R_W = 7168
        R_lo = PA.take(R_W)
        cur = {"wslots": None, "wlimit": 0, "brel": None, "esink": None}
        ident = T("ident", PA.take(64), 128, BF16)
        ones = T("ones", PA.take(32), 64, BF16)
        tokvalid = T("tokvalid", PA.take(NBLK_CORE), NBLK_CORE, F32)
        kbias = T("kbias", PA.take(NBLK_CORE), NBLK_CORE, F32)
        bg = T("bg", PA.take(16), 16, F32)
        cw = T("cw", PA.take(12), 12, F32)
        cbs = T("cbs", PA.take(4), 4, F32)
        fcw = T("fcw", PA.take(6 * NJ), 6 * NJ, F32)
        fcb = T("fcb", PA.take(2 * NJ), 2 * NJ, F32)
        class Stat:
            def __init__(self, i):
                self.bn = T(f"stat_bn{i}", PA.take(16), 16, F32)
                self.mv = T(f"stat_mv{i}", PA.take(8), 8, F32)
                self.rs = T(f"stat_rs{i}", PA.take(8), 8, F32)

        stat = [Stat(i) for i in range(3)]
        stat0 = [Stat(i + 3) for i in range(3)]
        OV = PA.p
        OVW = AW - OV

        bank = [S.buf(f"bank{i}") for i in range(8)]
        bank[4].strict = True
        bank[5].strict = True

        def banks_of(c0, c1):
            return [bank[i] for i in range(c0 // 512, (c1 - 1) // 512 + 1)]

        lane_ws = [new_lane(f"l_ws{i}") for i in range(NSLOT)]
        lane_x = [new_lane(f"l_x{i}") for i in range(4)]
        lane_o = [new_lane(f"l_o{i}") for i in range(4)]
        lane_g = new_lane("l_g")
        lane_b = new_lane("l_b")
        lane_g0 = new_lane("l_g0")
        lane_b0 = new_lane("l_b0")
        lane_wo = new_lane("l_wo")
        lane_wd = new_lane("l_wd")

        def const_load(t, src, eng="sp"):
            ln = new_lane("l_c_" + t.buf.name)
            S.op(eng, lambda e, t=t, src=src: e.dma_start(out=t.ap, in_=src), writes=[t.buf], lane=ln,
                 name="const_" + t.buf.name)

        const_load(tokvalid, tokvalid_d)
        S.op("pool", lambda e: e.memset(ident.ap, 0.0), writes=[ident.buf], name="ident0")
        S.op("pool", lambda e: e.affine_select(out=ident.ap, in_=ident.ap, pattern=[[-1, 128]],
                                               compare_op=ALU.not_equal, fill=1.0, base=0, channel_multiplier=1),
             reads=[ident.buf], writes=[ident.buf], name="ident1")
        S.op("pool", lambda e: e.memset(ones.ap, 1.0), writes=[ones.buf], name="ones")

        def late_consts():
            const_load(kbias, kbias_d)
            const_load(bg, bg_d)
            const_load(cw, cw_d)
            const_load(cbs, cb_d)
            const_load(fcw, fcw_d)
            const_load(fcb, fcb_d)

        def region_b(ch):
            ab = Alloc(R_lo, R_lo + R_W)
            cur["wslots"] = [T(f"wslot{ch}_{i}", ab.take(512), 1024, BF16) for i in range(NSLOT)]
            cur["brel"] = T(f"brel{ch}", ab.take(3072), 3072, F32)
            cur["esink"] = T(f"esink{ch}", ab.take(512), 512, F32)
            cur["wlimit"] = (ch + 1) * W_PER_CHUNK
            brel, esink = cur["brel"], cur["esink"]
            const_load(brel, brel_d)
            const_load(esink, sinkx_d)
            S.op("act", lambda e: e.activation(out=esink.ap, in_=esink.ap, func=AF.Exp), reads=[esink.buf],
                 writes=[esink.buf], name="esink_exp")

        wseq = []
        for _ch in range(NCH):
            wseq += [(win_d[0], 128, 8), (win_d[1], 128, 8)] + [(win_d[2 + c], 128, 8) for c in range(4)]
            for j in range(4):
                wseq += [(win_d[6 + 3 * j + 1], 128, 8), (win_d[6 + 3 * j + 2], 128, 8), (win_d[6 + 3 * j], 128, 8)]
            for f in range(8):
                wseq += [(win_d[18 + 2 * f], 128, 8), (win_d[18 + 2 * f + 1], 128, 8), (wa_d[f], 128, 4), (wc_d[f], 128, 4)]
            wseq += [(wup_d[ci], 128, 8) for ci in range(2 * NJ)]
        ring = {"cur": 0, "issued": 0}
        W_PER_CHUNK = len(wseq) // NCH

        def load_w(src_ap=None, parts=128, nk=8):
            n = ring["cur"]
            ring["cur"] += 1
            assert wseq[n][1] == parts and wseq[n][2] == nk, (n, wseq[n][1:], parts, nk)
            wslots = cur["wslots"]
            while ring["issued"] < min(len(wseq), n + NSLOT, cur["wlimit"]):
                m = ring["issued"]
                ring["issued"] += 1
                src, mp, mk = wseq[m]
                tm = wslots[m % NSLOT]
                dst = tm.ap[0:mp, 0:mk * 128].rearrange("p (k f) -> p k f", k=mk)
                S.op("pool", lambda e, dst=dst, src=src: e.dma_start(out=dst, in_=src, max_dma_last_dim=8192),
                     writes=[tm.buf], lane=lane_ws[m % NSLOT], name="wload")
            t = wslots[n % NSLOT]
            return t, t.ap.rearrange("p (k f) -> p k f", k=8)

        rr = {"i": 0}

        def region():
            r = rr["i"] % 2
            rr["i"] += 1
            return r * 1536

        def evac_engine():
            return "act"

        def ln_load_gb(which):
            S.op("sp", lambda e: e.dma_start(out=gtile.ap, in_=lng_d[which].partition_broadcast(128)),
                 writes=[gtile.buf], lane=lane_g, name="ld_g")
            S.op("sp", lambda e: e.dma_start(out=btile.ap, in_=lnb_d[which].partition_broadcast(128)),
                 writes=[btile.buf], lane=lane_b, name="ld_b")

        class Pipe:
            def __init__(self, n_items, stages):
                self.n = n_items
                self.stages = stages
                self.nsteps = n_items + len(stages) - 1

            def step(self, s):
                for k, st in enumerate(self.stages):
                    t = s - k
                    if 0 <= t < self.n:
                        st(t)

            def run(self):
                for s in range(self.nsteps):
                    self.step(s)

        def pipeline(n_items, stages):
            Pipe(n_items, stages).run()

        def ln_stats_a(src, st, junk_ap, junk_buf):
            sa = st.bn.ap
            jb = junk_buf if isinstance(junk_buf, list) else [junk_buf]
            S.op("act", lambda e: e.activation(out=junk_ap, in_=src.ap, func=AF.Identity, accum_out=sa[:, 0:1]),
                 reads=[src.buf], writes=[st.bn.buf] + jb, name="ln_sum")
            S.op("act", lambda e: e.activation(out=junk_ap, in_=src.ap, func=AF.Square, accum_out=sa[:, 1:2]),
                 reads=[src.buf, st.bn.buf], writes=[st.bn.buf] + jb, name="ln_sumsq")

        def ln_stats(src, st, from_act=False, gt=None):
            gt = gt or gtile
            sa = st.bn.ap
            mv = st.mv.ap
            if not from_act:
                S.op("dve", lambda e: e.bn_stats(out=sa[:, 0:6], in_=src.ap[:, 0:512]), reads=[src.buf], writes=[st.bn.buf])
                S.op("dve", lambda e: e.bn_stats(out=sa[:, 6:12], in_=src.ap[:, 512:1024]), reads=[src.buf, st.bn.buf],
                     writes=[st.bn.buf])
                S.op("dve", lambda e: e.bn_aggr(out=mv[:, 0:2], in_=sa[:, 0:12]), reads=[st.bn.buf], writes=[st.mv.buf])
            else:
                S.op("dve", lambda e: e.tensor_scalar(out=mv[:, 0:1], in0=sa[:, 0:1], scalar1=1.0 / D, scalar2=None,
                                                      op0=ALU.mult), reads=[st.bn.buf], writes=[st.mv.buf])
                S.op("dve", lambda e: e.tensor_scalar(out=sa[:, 2:3], in0=mv[:, 0:1], scalar1=mv[:, 0:1], scalar2=-1.0,
                                                      op0=ALU.mult, op1=ALU.mult), reads=[st.mv.buf], writes=[st.bn.buf])
                S.op("dve", lambda e: e.tensor_scalar(out=mv[:, 1:2], in0=sa[:, 1:2], scalar1=1.0 / D,
                                                      scalar2=sa[:, 2:3], op0=ALU.mult, op1=ALU.add),
                     reads=[st.bn.buf, st.mv.buf], writes=[st.mv.buf])
            S.op("dve", lambda e: e.scalar_tensor_tensor(out=src.ap, in0=src.ap, scalar=mv[:, 0:1], in1=gt.ap,
                                                         op0=ALU.subtract, op1=ALU.mult),
                 reads=[src.buf, st.mv.buf, gt.buf], writes=[src.buf])
            S.op("act", lambda e: e.activation(out=st.rs.ap[:, 0:1], in_=mv[:, 1:2], func=AF.Sqrt, bias=EPS, scale=1.0),
                 reads=[st.mv.buf], writes=[st.rs.buf])

        def ln_norm(src, dst, st, hb=None, valid_ap=None, bt=None):
            bt = bt or btile
            rs = st.rs.ap
            S.op("dve", lambda e: e.reciprocal(out=rs[:, 1:2], in_=rs[:, 0:1]), reads=[st.rs.buf], writes=[st.rs.buf])
            S.op("dve", lambda e: e.scalar_tensor_tensor(out=dst.ap, in0=src.ap, scalar=rs[:, 1:2], in1=bt.ap,
                                                         op0=ALU.mult, op1=ALU.add),
                 reads=[src.buf, st.rs.buf, bt.buf], writes=[dst.buf])
            if hb is not None:
                S.op("act", lambda e: e.activation(out=hb.ap, in_=dst.ap, func=AF.Identity, scale=valid_ap),
                     reads=[dst.buf, tokvalid.buf], writes=[hb.buf])

        def tr_mm(hb, pb):
            psb = ps[:, pb * 512:(pb + 1) * 512].bitcast(BF16)
            for kc in range(8):
                S.op("pe", lambda e, kc=kc: e.transpose(out=psb[:, kc * 128:(kc + 1) * 128],
                                                        in_=hb.ap[:, kc * 128:(kc + 1) * 128], identity=ident.ap),
                     reads=[hb.buf, ident.buf], writes=[bank[pb]])

        def tr_evac(dstT_ap, dst_buf, col0, pb):
            psb = ps[:, pb * 512:(pb + 1) * 512].bitcast(BF16)
            d3 = dstT_ap.rearrange("p (k f) -> p k f", k=8)[:, :, col0:col0 + 128]
            S.op("act", lambda e: e.activation(out=d3, in_=psb.rearrange("p (k f) -> p k f", k=8), func=AF.Copy),
                 reads=[bank[pb]], writes=[dst_buf])

        class TiledT:
            def __init__(self, name, lo):
                self.ap = b16ap(lo, 8 * TE)
                self.bufs = S.buf_group([f"{name}_{i}" for i in range(NBE)], lo, lo + 8 * TE // 2)

            def cols(self, c0, c1):
                return [self.bufs[i] for i in range(c0 // 128, (c1 - 1) // 128 + 1)]

        def proj_fm(w3, wbuf, src3, srcbufs, c_lo, ncols, r0, nk=8, kparts=128, src_k=None):
            g0 = 0
            while g0 < ncols:
                g1 = min(g0 + 512, ncols)
                for k in range(nk):
                    S.op("pe", lambda e, k=k, g0=g0, g1=g1: e.matmul(
                        out=ps[:, r0 + g0:r0 + g1], lhsT=w3[0:kparts, k, :],
                        rhs=src3[0:kparts, k, c_lo + g0:c_lo + g1], start=(k == 0), stop=(k == nk - 1)),
                        reads=[wbuf] + (srcbufs(c_lo + g0, c_lo + g1) if callable(srcbufs) else srcbufs),
                        writes=banks_of(r0 + g0, r0 + g1), name="proj")
                g0 = g1

        def do_chunk(ch):
            blk0 = ch * NT
            row0 = ch * TC
            hT = TiledT(f"hT{ch}", hT_lo)
            hT3 = hT.ap.rearrange("p (k f) -> p k f", k=8)
            A0 = Alloc(R_lo, R_lo + R_W)
            NX = 4
            xt = [T(f"xt{ch}_{i}", A0.take(1024), 1024, F32) for i in range(NX)]
            hb0 = [T(f"hb0{ch}_{i}", A0.take(512), 1024, BF16) for i in range(2)]
            g0tile = T(f"g0tile{ch}", A0.take(1024), 1024, F32)
            b0tile = T(f"b0tile{ch}", A0.take(1024), 1024, F32)
            junk_ap = ps[:, 4 * 512:6 * 512]
            S.op("sp", lambda e: e.dma_start(out=g0tile.ap, in_=lng_d[0].partition_broadcast(128)),
                 writes=[g0tile.buf], lane=lane_g0, name="ld_g0")
            S.op("sp", lambda e: e.dma_start(out=b0tile.ap, in_=lnb_d[0].partition_broadcast(128)),
                 writes=[b0tile.buf], lane=lane_b0, name="ld_b0")

            p0_order = list(range(NBE)) if ch == 0 else [0, 1, NT + 2, NT + 3] + list(range(2, NT + 2))

            def p0_dst(eb, n):
                return hres[eb - 1] if 1 <= eb <= NT + 2 else xt[n % NX]

            def p0_load(n):
                eb = p0_order[n]
                S.op("sp", lambda e: e.dma_start(out=xt[n % NX].ap, in_=x_d[row0 + eb * 128:row0 + (eb + 1) * 128, :]),
                     writes=[xt[n % NX].buf], lane=lane_x[n % NX], name="ld_x")

            def use_act(n):
                return True if ch > 0 else (n % 2 == 1)

            def p0_act(n):
                if use_act(n):
                    ln_stats_a(xt[n % NX], stat0[n % 3], junk_ap, [bank[4], bank[5]])

            def p0_norm(n):
                eb = p0_order[n]
                ln_norm(xt[n % NX], p0_dst(eb, n), stat0[n % 3], hb=hb0[n % 2],
                        valid_ap=tokvalid.ap[:, blk0 + eb:blk0 + eb + 1], bt=b0tile)

            p0 = Pipe(NBE, [
                p0_load,
                p0_act,
                lambda n: ln_stats(xt[n % NX], stat0[n % 3], from_act=use_act(n), gt=g0tile),
                p0_norm,
                lambda n: tr_mm(hb0[n % 2], 6 + n % 2),
                lambda n: tr_evac(hT.ap, hT.bufs[p0_order[n]], p0_order[n] * 128, 6 + n % 2),
            ])
            yield p0

            region_b(ch)
            brel, esink = cur["brel"], cur["esink"]
            OA = Alloc(OV, AW)
            mT = [T(f"mT{ch}_{f}", OA.take(640), 1280, BF16) for f in range(8)]
            attT = T(f"attT{ch}", OA.take(4 * TQ // 2), 4 * TQ, BF16)
            attT3 = attT.ap.rearrange("p (g q) -> p g q", g=4)
            convT = T(f"convT{ch}", OA.take(4 * TQ // 2), 4 * TQ, BF16)
            convT3 = convT.ap.rearrange("p (j q) -> p j q", j=4)
            SA = OA.take(7184)
            SB = OA.take(4096)
            SC = OA.take(3 * 1032)
            if ch == 0:
                dbg("h_t1", hres[1].ap, [hres[1].buf])
                dbg("hT", hT.ap, hT.bufs, BF16)
            if ch == 0:
                late_consts()
            for f in range(8):
                S.op("dve", lambda e, f=f: e.memset(mT[f].ap[:, 0:QLO - 128], 0.0), writes=[mT[f].buf], name="mT0")
                S.op("dve", lambda e, f=f: e.memset(mT[f].ap[:, QLO - 128 + TQ:1280], 0.0), writes=[mT[f].buf], name="mT0")

            A1 = Alloc(SA, SA + 7184 + 4096)
            kT = T(f"kT{ch}", A1.take(TE // 2), TE, BF16)
            vt = T(f"v{ch}", A1.take(TE // 2), TE, BF16)
            vt3 = vt.ap.rearrange("p (b f) -> p b f", b=NBE)
            qT = T(f"qT{ch}", A1.take(4 * TQ // 2), 4 * TQ, BF16)
            qT3 = qT.ap.rearrange("p (c q) -> p c q", c=4)
            tt = [T(f"tt{ch}_{i}", A1.take(1024), 1024, F32) for i in range(2)]
            PTr = [T(f"PT{ch}_{i}", A1.take(512), 1024, BF16) for i in range(3)]
            dtmp = [T(f"dtmp{ch}_{i}", A1.take(512), 512, F32) for i in range(2)]

            wt, w3 = load_w(win_d[0])
            r0 = region()
            proj_fm(w3, wt.buf, hT3, hT.cols, 0, TE, r0)
            S.op("act", lambda e, r0=r0: e.activation(out=kT.ap, in_=ps[:, r0:r0 + TE], func=AF.Copy),
                 reads=banks_of(r0, r0 + TE), writes=[kT.buf], name="evac_k")
            wt, w3 = load_w(win_d[1])
            r0 = region()
            for eb in range(NBE):
                for k in range(8):
                    S.op("pe", lambda e, k=k, eb=eb, r0=r0, w3=w3: e.matmul(
                        out=ps[:, r0 + eb * 128:r0 + (eb + 1) * 128], lhsT=hT3[:, k, eb * 128:(eb + 1) * 128],
                        rhs=w3[:, k, :], start=(k == 0), stop=(k == 7)),
                        reads=[wt.buf, hT.bufs[eb]], writes=banks_of(r0 + eb * 128, r0 + (eb + 1) * 128), name="vproj")
            S.op("dve", lambda e, r0=r0: e.tensor_copy(out=vt.ap, in_=ps[:, r0:r0 + TE]),
                 reads=banks_of(r0, r0 + TE), writes=[vt.buf], name="evac_v")
            for c in range(4):
                wt, w3 = load_w(win_d[2 + c])
                r0 = region()
                proj_fm(w3, wt.buf, hT3, hT.cols, QLO, TQ, r0)
                eng = "act" if c % 2 == 0 else "dve"
                if eng == "act":
                    S.op("act", lambda e, r0=r0, c=c: e.activation(out=qT3[:, c, :], in_=ps[:, r0:r0 + TQ], func=AF.Copy),
                         reads=banks_of(r0, r0 + TQ), writes=[qT.buf], name="evac_q")
                else:
                    S.op("dve", lambda e, r0=r0, c=c: e.tensor_copy(out=qT3[:, c, :], in_=ps[:, r0:r0 + TQ]),
                         reads=banks_of(r0, r0 + TQ), writes=[qT.buf], name="evac_q")

            if ch == 0:
                dbg("kT", kT.ap, [kT.buf], BF16)
                dbg("vt", vt.ap, [vt.buf], BF16)
                dbg("qT", qT.ap, [qT.buf], BF16)
            qblocks = [(1, 127, 1, 0)] + [(eb, 0, 128, (eb - 2) * 128 + 1) for eb in range(2, NT + 2)] + \
                      [(NT + 2, 0, 1, TQ - 1)]
            brel4 = brel.ap.rearrange("p (j h q) -> p j h q", j=3, h=8)
            esink3 = esink.ap.rearrange("p (g q) -> p g q", g=4)
            ttc = {"i": 0}

            def at_qk(m):
                b, j = m // 3, m % 3
                eb, ql0, qn, qc0 = qblocks[b]
                n4 = 4 * qn
                kb = eb - 1 + j
                pr = m % 2
                for kvh in range(2):
                    p0 = kvh * 64
                    bk = 2 * pr + kvh
                    S.op("pe", lambda e, bk=bk, p0=p0: e.matmul(
                        out=ps[:, bk * 512:bk * 512 + n4].rearrange("p (g q) -> p g q", g=4),
                        lhsT=kT.ap[p0:p0 + 64, kb * 128:(kb + 1) * 128],
                        rhs=qT3[p0:p0 + 64, :, qc0:qc0 + qn], start=True, stop=True),
                        reads=[kT.buf, qT.buf], writes=[bank[bk]], name="qk")

            def at_exp(m):
                b, j = m // 3, m % 3
                eb, ql0, qn, qc0 = qblocks[b]
                n4 = 4 * qn
                kb = eb - 1 + j
                pr = m % 2
                tb = tt[m % 2]
                pt = PTr[m % 3]
                S.op("dve", lambda e: e.scalar_tensor_tensor(
                    out=tb.ap[:, 0:2 * n4].rearrange("p (k g q) -> p k g q", k=2, g=4),
                    in0=ps[:, 2 * pr * 512:(2 * pr + 2) * 512].rearrange("p (k c) -> p k c", k=2)[:, :, 0:n4]
                    .rearrange("p k (g q) -> p k g q", g=4),
                    scalar=0.125,
                    in1=brel4[:, j, :, ql0:ql0 + qn].rearrange("p (k g) q -> p k g q", k=2),
                    op0=ALU.mult, op1=ALU.add),
                    reads=[bank[2 * pr], bank[2 * pr + 1], brel.buf], writes=[tb.buf], name="sbias")
                S.op("act", lambda e: e.activation(
                    out=pt.ap[:, 0:2 * n4], in_=tb.ap[:, 0:2 * n4], func=AF.Exp,
                    bias=kbias.ap[:, blk0 + kb:blk0 + kb + 1], scale=1.0),
                    reads=[tb.buf, kbias.buf], writes=[pt.buf], name="exp")

            def at_pv(m):
                b, j = m // 3, m % 3
                eb, ql0, qn, qc0 = qblocks[b]
                i = b % 2
                n4 = 4 * qn
                kb = eb - 1 + j
                pt = PTr[m % 3]
                bo, bd = 4 + 2 * i, 5 + 2 * i
                for kvh in range(2):
                    p0 = kvh * 64
                    S.op("pe", lambda e, p0=p0, kvh=kvh: e.matmul(
                        out=ps[p0:p0 + 64, bo * 512:bo * 512 + n4], lhsT=vt3[:, kb, p0:p0 + 64],
                        rhs=pt.ap[:, kvh * n4:(kvh + 1) * n4], start=(j == 0), stop=(j == 2)),
                        reads=[vt.buf, pt.buf], writes=[bank[bo]], name="pv")
                for kvh in range(2):
                    p0 = kvh * 64
                    S.op("pe", lambda e, p0=p0, kvh=kvh: e.matmul(
                        out=ps[p0:p0 + 64, bd * 512:bd * 512 + n4], lhsT=ones.ap,
                        rhs=pt.ap[:, kvh * n4:(kvh + 1) * n4], start=(j == 0), stop=(j == 2)),
                        reads=[ones.buf, pt.buf], writes=[bank[bd]], name="den")

            def at_norm(m):
                b, j = m // 3, m % 3
                if j != 2:
                    return
                eb, ql0, qn, qc0 = qblocks[b]
                i = b % 2
                n4 = 4 * qn
                bo, bd = 4 + 2 * i, 5 + 2 * i
                S.op("dve", lambda e: e.tensor_tensor(
                    out=dtmp[i].ap[:, 0:n4].rearrange("p (g q) -> p g q", g=4),
                    in0=ps[:, bd * 512:bd * 512 + n4].rearrange("p (g q) -> p g q", g=4),
                    in1=esink3[:, :, ql0:ql0 + qn], op=ALU.add),
                    reads=[bank[bd], esink.buf], writes=[dtmp[i].buf], name="den_sink")
                S.op("act", lambda e: e.activation(out=dtmp[i].ap[:, 0:n4], in_=dtmp[i].ap[:, 0:n4], func=AF.Ln),
                     reads=[dtmp[i].buf], writes=[dtmp[i].buf], name="ln_den")
                S.op("act", lambda e: e.activation(out=dtmp[i].ap[:, 0:n4], in_=dtmp[i].ap[:, 0:n4], func=AF.Exp, scale=-1.0),
                     reads=[dtmp[i].buf], writes=[dtmp[i].buf], name="recip_den")

            def at_norm2(m):
                b, j = m // 3, m % 3
                if j != 2:
                    return
                eb, ql0, qn, qc0 = qblocks[b]
                i = b % 2
                n4 = 4 * qn
                bo, bd = 4 + 2 * i, 5 + 2 * i
                S.op("dve", lambda e: e.tensor_tensor(
                    out=attT3[:, :, qc0:qc0 + qn],
                    in0=ps[:, bo * 512:bo * 512 + n4].rearrange("p (g q) -> p g q", g=4),
                    in1=dtmp[i].ap[:, 0:n4].rearrange("p (g q) -> p g q", g=4), op=ALU.mult),
                    reads=[bank[bo], dtmp[i].buf], writes=[attT.buf], name="att_norm")

            pipeline(3 * len(qblocks), [at_qk, at_exp, at_pv, at_norm, at_norm2])

            wo = T(f"wo{ch}", SB, 8 * 1024, BF16)
            wo3 = wo.ap.rearrange("p (k f) -> p k f", k=8)
            S.op("pool", lambda e: e.dma_start(out=wo3, in_=wo_d, max_dma_last_dim=8192), writes=[wo.buf],
                 lane=lane_wo, name="ld_wo")

            if ch == 0:
                dbg("attT", attT.ap, [attT.buf], BF16)
            ccs = T(f"ccs{ch}", SC, TP, F32)
            pp = T(f"pp{ch}", SC + 1032, TP, F32)
            acc = T(f"acc{ch}", SC + 2064, TQ, F32)
            cbev = T(f"cbev{ch}", SA, TQ, F32)
            for j in range(4):
                wt, w3 = load_w(win_d[6 + 3 * j + 1])
                rc = region()
                proj_fm(w3, wt.buf, hT3, hT.cols, QLO - 1, TP, rc)
                S.op("act", lambda e, rc=rc: e.activation(out=ccs.ap, in_=ps[:, rc:rc + TP], func=AF.Copy),
                     reads=banks_of(rc, rc + TP), writes=[ccs.buf], name="evac_cc")
                wt, w3 = load_w(win_d[6 + 3 * j + 2])
                rx = region()
                proj_fm(w3, wt.buf, hT3, hT.cols, QLO - 1, TP, rx)
                S.op("dve", lambda e, rx=rx: e.tensor_tensor(out=pp.ap, in0=ps[:, rx:rx + TP], in1=ccs.ap, op=ALU.mult),
                     reads=banks_of(rx, rx + TP) + [ccs.buf], writes=[pp.buf], name="p_mul")
                wt, w3 = load_w(win_d[6 + 3 * j + 0])
                rb = region()
                proj_fm(w3, wt.buf, hT3, hT.cols, QLO, TQ, rb)
                S.op("act", lambda e, rb=rb: e.activation(out=cbev.ap, in_=ps[:, rb:rb + TQ], func=AF.Copy),
                     reads=banks_of(rb, rb + TQ), writes=[cbev.buf], name="evac_cb")
                S.op("act", lambda e, j=j: e.activation(out=acc.ap, in_=pp.ap[:, 1:1 + TQ], func=AF.Identity,
                                                        bias=cbs.ap[:, j:j + 1], scale=cw.ap[:, 3 * j + 1:3 * j + 2]),
                     reads=[pp.buf, cbs.buf, cw.buf], writes=[acc.buf], name="conv1")
                S.op("dve", lambda e, j=j: e.scalar_tensor_tensor(out=acc.ap, in0=pp.ap[:, 0:TQ],
                                                                  scalar=cw.ap[:, 3 * j:3 * j + 1], in1=acc.ap,
                                                                  op0=ALU.mult, op1=ALU.add),
                     reads=[pp.buf, cw.buf, acc.buf], writes=[acc.buf], name="conv0")
                S.op("dve", lambda e, j=j: e.scalar_tensor_tensor(out=acc.ap, in0=pp.ap[:, 2:2 + TQ],
                                                                  scalar=cw.ap[:, 3 * j + 2:3 * j + 3], in1=acc.ap,
                                                                  op0=ALU.mult, op1=ALU.add),
                     reads=[pp.buf, cw.buf, acc.buf], writes=[acc.buf], name="conv2")
                S.op("dve", lambda e, j=j: e.tensor_tensor(out=convT3[:, j, :], in0=cbev.ap, in1=acc.ap, op=ALU.mult),
                     reads=[cbev.buf, acc.buf], writes=[convT.buf], name="conv_gate")

            if ch == 0:
                dbg("convT", convT.ap, [convT.buf], BF16)
            A2 = Alloc(SA, SA + 7184)
            gab = [T(f"ga{ch}_{i}", A2.take(TQ), TQ, F32) for i in range(2)]
            gcb = [T(f"gc{ch}_{i}", A2.take(TQ), TQ, F32) for i in range(2)]
            t1b = [T(f"t1{ch}_{i}", A2.take(TQ), TQ, F32) for i in range(2)]
            for f in range(8):
                i = f % 2
                wt, w3 = load_w(win_d[18 + 2 * f])
                ra = region()
                proj_fm(w3, wt.buf, hT3, hT.cols, QLO, TQ, ra)
                S.op("act", lambda e, ra=ra, f=f, i=i: e.activation(out=gab[i].ap, in_=ps[:, ra:ra + TQ], func=AF.Sigmoid,
                                                                    bias=bg.ap[:, f:f + 1], scale=1.0),
                     reads=banks_of(ra, ra + TQ) + [bg.buf], writes=[gab[i].buf], name="sig_a")
                wt, w3 = load_w(win_d[18 + 2 * f + 1])
                rc = region()
                proj_fm(w3, wt.buf, hT3, hT.cols, QLO, TQ, rc)
                S.op("act", lambda e, rc=rc, f=f, i=i: e.activation(out=gcb[i].ap, in_=ps[:, rc:rc + TQ], func=AF.Sigmoid,
                                                                    bias=bg.ap[:, 8 + f:9 + f], scale=1.0),
                     reads=banks_of(rc, rc + TQ) + [bg.buf], writes=[gcb[i].buf], name="sig_c")
                wt, w3 = load_w(wa_d[f], nk=4)
                ra = region()
                proj_fm(w3, wt.buf, attT3, [attT.buf], 0, TQ, ra, nk=4)
                S.op("dve", lambda e, ra=ra, i=i: e.tensor_tensor(out=t1b[i].ap, in0=ps[:, ra:ra + TQ], in1=gab[i].ap,
                                                                  op=ALU.mult),
                     reads=banks_of(ra, ra + TQ) + [gab[i].buf], writes=[t1b[i].buf], name="gate_a")
                wt, w3 = load_w(wc_d[f], nk=4)
                rc = region()
                proj_fm(w3, wt.buf, convT3, [convT.buf], 0, TQ, rc, nk=4)
                S.op("dve", lambda e, rc=rc, i=i: e.tensor_tensor(out=gcb[i].ap, in0=ps[:, rc:rc + TQ], in1=gcb[i].ap,
                                                                  op=ALU.mult),
                     reads=banks_of(rc, rc + TQ) + [gcb[i].buf], writes=[gcb[i].buf], name="gate_c")
                S.op("dve", lambda e, f=f, i=i: e.tensor_tensor(out=mT[f].ap[:, QLO - 128:QLO - 128 + TQ], in0=t1b[i].ap,
                                                                 in1=gcb[i].ap, op=ALU.add),
                     reads=[t1b[i].buf, gcb[i].buf], writes=[mT[f].buf], name="merge")

            if ch == 0:
                dbg("mT0", mT[0].ap, [mT[0].buf], BF16)
                dbg("mT7", mT[7].ap, [mT[7].buf], BF16)
            A3 = Alloc(SA, SA + 7184)
            tmp2 = [T(f"tmp2{ch}_{i}", A3.take(1024), 1024, F32) for i in range(4)]
            hb2 = [T(f"hb2{ch}_{i}", A3.take(512), 1024, BF16) for i in range(2)]
            junk2 = T(f"junk2{ch}", A3.take(512), 1024, BF16)
            h2T = TiledT(f"h2T{ch}", hT_lo)
            h2T3 = h2T.ap.rearrange("p (k f) -> p k f", k=8)
            ln_load_gb(1)

            p2_order = [0, NT + 1] + list(range(1, NT + 1))

            def p2_mm(n):
                t = p2_order[n]
                r0 = (n % 2) * 1024
                for half in range(2):
                    for k in range(8):
                        S.op("pe", lambda e, k=k, half=half: e.matmul(
                            out=ps[:, r0 + half * 512:r0 + (half + 1) * 512], lhsT=mT[k].ap[:, t * 128:(t + 1) * 128],
                            rhs=wo3[:, k, half * 512:(half + 1) * 512], start=(k == 0), stop=(k == 7)),
                            reads=[mT[k].buf, wo.buf], writes=[bank[r0 // 512 + half]], name="wo_mm")

            def p2_resid(n):
                t = p2_order[n]
                r0 = (n % 2) * 1024
                i = n % 4
                S.op("dve", lambda e: e.scalar_tensor_tensor(
                    out=tmp2[i].ap, in0=hres[t].ap, scalar=ALPHA, in1=ps[:, r0:r0 + 1024], op0=ALU.mult, op1=ALU.add),
                    reads=[hres[t].buf] + banks_of(r0, r0 + 1024), writes=[tmp2[i].buf], name="resid1")

            def p2_norm(n):
                t = p2_order[n]
                ln_norm(tmp2[n % 4], hres[t], stat[n % 3], hb=hb2[n % 2],
                        valid_ap=tokvalid.ap[:, blk0 + t + 1:blk0 + t + 2])

            pipeline(NT + 2, [
                p2_mm,
                p2_resid,
                lambda n: ln_stats_a(tmp2[n % 4], stat[n % 3], ps[:, 4 * 512:6 * 512], [bank[4], bank[5]]),
                lambda n: ln_stats(tmp2[n % 4], stat[n % 3], from_act=True),
                p2_norm,
                lambda n: tr_mm(hb2[n % 2], 6 + n % 2),
                lambda n: tr_evac(h2T.ap, h2T.bufs[p2_order[n] + 1], (p2_order[n] + 1) * 128, 6 + n % 2),
            ])

            if ch == 0:
                dbg("h2_t1", hres[1].ap, [hres[1].buf])
                dbg("h2T", h2T.ap, h2T.bufs[1:NT + 3], BF16)
            OB = Alloc(OV, AW)
            gT = [T(f"gT{ch}_{j}", OB.take(512), 1024, BF16) for j in range(NJ)]
            wd_lo = OB.take(NJ * 512)
            accs = [[T(f"facc{ch}_{i}_{s}", OB.take(1024), 1024, F32) for s in range(2)] for i in range(2)]
            o_lo = accs[0][0].buf.lo
            wdt = T(f"wd{ch}", wd_lo, NJ * 1024, BF16)
            wd3 = wdt.ap.rearrange("p (k f) -> p k f", k=NJ)
            for j in range(NJ):
                i = j % 2
                S.op("pool", lambda e, j=j: e.dma_start(out=wd3[:, j, :], in_=wd_d[:, j, :], max_dma_last_dim=8192),
                     writes=[wdt.buf], lane=lane_wd, name="ld_wd")
                for s in range(2):
                    ci = 2 * j + s
                    wt, w3 = load_w(wup_d[ci])
                    r0 = (ci % 3) * 1024
                    hbk = 6 + ci % 2
                    h0 = hbk * 512 + 2 * (ci // 2 % 64)
                    proj_fm(w3, wt.buf, h2T3, h2T.cols, QLO + 1, TC, r0)
                    for k in range(8):
                        S.op("pe", lambda e, k=k, w3=w3, h0=h0: e.matmul(
                            out=ps[:, h0:h0 + 2], lhsT=w3[:, k, :], rhs=h2T3[:, k, QLO:QLO + TQ:TQ - 1],
                            start=(k == 0), stop=(k == 7)),
                            reads=[wt.buf, h2T.bufs[1], h2T.bufs[NT + 2]], writes=[bank[hbk]], name="proj_halo")
                    a = accs[i][s]
                    rb = banks_of(r0, r0 + TC)
                    S.op("act", lambda e, r0=r0, a=a, ci=ci: e.activation(
                        out=a.ap, in_=ps[:, r0:r0 + TC], func=AF.Identity, bias=fcb.ap[:, ci:ci + 1],
                        scale=fcw.ap[:, 3 * ci + 1:3 * ci + 2]),
                        reads=rb + [fcb.buf, fcw.buf], writes=[a.buf], name="fconv1")
                    S.op("act", lambda e, a=a, ci=ci, h0=h0: e.activation(
                        out=a.ap[:, 0:1], in_=ps[:, h0:h0 + 1], func=AF.Identity, bias=a.ap[:, 0:1],
                        scale=fcw.ap[:, 3 * ci:3 * ci + 1]),
                        reads=[bank[hbk], fcw.buf, a.buf], writes=[a.buf], name="fedgeL")
                    S.op("act", lambda e, a=a, ci=ci, h0=h0: e.activation(
                        out=a.ap[:, TC - 1:TC], in_=ps[:, h0 + 1:h0 + 2], func=AF.Identity, bias=a.ap[:, TC - 1:TC],
                        scale=fcw.ap[:, 3 * ci + 2:3 * ci + 3]),
                        reads=[bank[hbk], fcw.buf, a.buf], writes=[a.buf], name="fedgeR")
                    S.op("dve", lambda e, r0=r0, a=a, ci=ci: e.scalar_tensor_tensor(
                        out=a.ap[:, 1:TC], in0=ps[:, r0:r0 + TC - 1], scalar=fcw.ap[:, 3 * ci:3 * ci + 1],
                        in1=a.ap[:, 1:TC], op0=ALU.mult, op1=ALU.add),
                        reads=rb + [fcw.buf, a.buf], writes=[a.buf], name="fconv0")
                    S.op("dve", lambda e, r0=r0, a=a, ci=ci: e.scalar_tensor_tensor(
                        out=a.ap[:, 0:TC - 1], in0=ps[:, r0 + 1:r0 + TC], scalar=fcw.ap[:, 3 * ci + 2:3 * ci + 3],
                        in1=a.ap[:, 0:TC - 1], op0=ALU.mult, op1=ALU.add),
                        reads=rb + [fcw.buf, a.buf], writes=[a.buf], name="fconv2")
                aa, uu = accs[i][0], accs[i][1]
                S.op("act", lambda e, aa=aa: e.activation(out=aa.ap, in_=aa.ap, func=AF.Silu),
                     reads=[aa.buf], writes=[aa.buf], name="silu")
                S.op("dve", lambda e, aa=aa, uu=uu, j=j: e.tensor_tensor(out=gT[j].ap, in0=aa.ap, in1=uu.ap, op=ALU.mult),
                     reads=[aa.buf, uu.buf], writes=[gT[j].buf], name="glu")
            if ch == 0:
                dbg("gT0", gT[0].ap, [gT[0].buf], BF16)
                dbg("gT21", gT[21].ap, [gT[21].buf], BF16)
            OT = Alloc(o_lo, AW)
            tmp4 = [T(f"tmp4{ch}_{i}", OT.take(1024), 1024, F32) for i in range(2)]
            ot = [T(f"ot{ch}_{i}", OT.take(1024), 1024, F32) for i in range(2)]
            ln_load_gb(2)
            out_ops = []

            def p4_mm(t):
                r0 = (t % 2) * 1024
                for half in range(2):
                    for k in range(NJ):
                        S.op("pe", lambda e, k=k, half=half: e.matmul(
                            out=ps[:, r0 + half * 512:r0 + (half + 1) * 512], lhsT=gT[k].ap[:, t * 128:(t + 1) * 128],
                            rhs=wd3[:, k, half * 512:(half + 1) * 512], start=(k == 0), stop=(k == NJ - 1)),
                            reads=[gT[k].buf, wdt.buf], writes=[bank[r0 // 512 + half]], name="wd_mm")

            act4 = (ch < NCH - 1)
            t4 = tmp4 + ot

            def p4_resid(t):
                r0 = (t % 2) * 1024
                src = t4[t % 4] if act4 else tmp4[t % 2]
                S.op("dve", lambda e: e.scalar_tensor_tensor(
                    out=src.ap, in0=hres[t + 1].ap, scalar=ALPHA, in1=ps[:, r0:r0 + 1024], op0=ALU.mult, op1=ALU.add),
                    reads=[hres[t + 1].buf] + banks_of(r0, r0 + 1024), writes=[src.buf], name="resid2")

            def p4_out(t, src):
                orow = ch * TC + t * 128
                o = S.op("sp", lambda e: e.dma_start(out=out_d[orow:orow + 128, :], in_=src.ap),
                         reads=[src.buf], lane=lane_o[t % 4 if act4 else t % 2], name="st_out")
                out_ops.append(o)

            if act4:
                junk4 = ps[:, 4 * 512:6 * 512]

                def p4_norm(t):
                    ln_norm(t4[t % 4], t4[t % 4], stat[t % 3])
                    p4_out(t, t4[t % 4])

                yield Pipe(NT, [
                    p4_mm,
                    p4_resid,
                    lambda t: ln_stats_a(t4[t % 4], stat[t % 3], junk4, [bank[4], bank[5]]),
                    lambda t: ln_stats(t4[t % 4], stat[t % 3], from_act=True),
                    p4_norm,
                ])
            else:
                def p4_stats(t):
                    p4_resid(t)
                    ln_stats(tmp4[t % 2], stat[t % 3])

                def p4_norm(t):
                    i = t % 2
                    ln_norm(tmp4[i], ot[i], stat[t % 3])
                    p4_out(t, ot[i])

                yield Pipe(NT, [p4_mm, p4_stats, p4_norm])
            out_ops_all.extend(out_ops)
        out_ops_all = []
        g0 = do_chunk(0)
        next(g0).run()
        p4 = next(g0)
        g1 = do_chunk(1)
        p0n = next(g1)
        j = 0
        for sidx in range(p4.nsteps):
            p4.step(sidx)
            for _ in range(2):
                if j < p0n.nsteps and j <= sidx + 6:
                    p0n.step(j)
                    j += 1
        while j < p0n.nsteps:
            p0n.step(j)
            j += 1
        for _ in g0:
            pass
        p4b = next(g1)
        p4b.run()
        for _ in g1:
            pass
        last_per_lane = {}
        for o in out_ops_all:
            last_per_lane[id(o.lane)] = o
        final = list(last_per_lane.values())

        S.emit(nc, es, final + dbg_list)
    return nc


def _rel_bucket(rel):
    half = 16
    max_exact = 8
    offset = np.where(rel > 0, half, 0)
    n = np.abs(rel)
    nf = np.maximum(n, 1).astype(np.float32)
    large = max_exact + (np.log(nf / max_exact) / math.log(128 / max_exact) * (half - max_exact)).astype(np.int32)
    large = np.minimum(large, half - 1)
    return offset + np.where(n < max_exact, n, large)


def _tile_w(w, col_idx):
    sub = w[:, col_idx]
    K = sub.shape[0]
    return np.ascontiguousarray(sub.reshape(K // 128, 128, -1).transpose(1, 0, 2))


_PROGRAM = None


def kernel(x, ln_in_g, ln_in_b, w_in, b_gates, attn_sink, rel_bias, conv_w, conv_b,
           w_att_branch, w_conv_branch, w_o, ln_mix_g, ln_mix_b,
           w_ffn_up, ffn_conv_w, ffn_conv_b, w_ffn_down, ln_ffn_g, ln_ffn_b):
    global _PROGRAM
    f32 = np.float32
    x = np.asarray(x, f32)
    w_in0 = np.asarray(w_in, f32)[0]
    ar = np.arange(128)
    cols = []
    cols.append(512 + ar)
    cols.append(640 + ar)
    for c in range(4):
        cols.append(np.concatenate([c * 64 + np.arange(64), (4 + c) * 64 + np.arange(64)]))
    for j in range(4):
        cols.append(768 + j * 128 + ar)
        cols.append(1280 + j * 128 + ar)
        cols.append(1792 + j * 128 + ar)
    for f in range(8):
        cols.append(2304 + f * 128 + ar)
        cols.append(3328 + f * 128 + ar)
    win = np.stack([_tile_w(w_in0, c) for c in cols])
    bgv = np.asarray(b_gates, f32)[0]
    bg = np.ascontiguousarray(bgv.reshape(16, 128).T)
    cwv = np.asarray(conv_w, f32)[0]
    cw = np.ascontiguousarray(cwv.reshape(3, 4, 128).transpose(2, 1, 0).reshape(128, 12))
    cbias = np.ascontiguousarray(np.asarray(conv_b, f32)[0].reshape(4, 128).T)
    wab = np.asarray(w_att_branch, f32)[0]
    wa = np.ascontiguousarray(wab.reshape(2, 4, 64, 8, 128).transpose(3, 0, 2, 1, 4).reshape(8, 128, 4, 128))
    wcb = np.asarray(w_conv_branch, f32)[0]
    wc = np.ascontiguousarray(wcb.reshape(4, 128, 8, 128).transpose(2, 1, 0, 3))
    wo = np.ascontiguousarray(np.asarray(w_o, f32)[0].reshape(8, 128, 1024).transpose(1, 0, 2))
    wup0 = np.asarray(w_ffn_up, f32)[0]
    upcols = []
    for j in range(NJ):
        upcols.append(j * 128 + ar)
        upcols.append(DFF + j * 128 + ar)
    wup = np.stack([_tile_w(wup0, c) for c in upcols])
    fw = np.asarray(ffn_conv_w, f32)[0]
    fb = np.asarray(ffn_conv_b, f32)[0]
    fcw = np.ascontiguousarray(np.stack([fw[:, c] for c in upcols]).transpose(2, 0, 1).reshape(128, 2 * NJ * 3))
    fcb = np.ascontiguousarray(np.stack([fb[c] for c in upcols]).T)
    wd = np.ascontiguousarray(np.asarray(w_ffn_down, f32)[0].reshape(NJ, 128, 1024).transpose(1, 0, 2))
    c_i = np.arange(128)[:, None]
    q_i = np.arange(128)[None, :]
    rb = np.asarray(rel_bias, f32)
    brel = np.empty((128, 3, 8, 128), f32)
    for j in range(3):
        rel = (j - 1) * 128 + c_i - q_i
        bt = rb[_rel_bucket(rel)]
        bt = np.where((np.abs(rel) <= 128)[:, :, None], bt, f32(NEG))
        brel[:, j] = bt.transpose(0, 2, 1)
    brel = np.ascontiguousarray(brel.reshape(128, 3 * 8 * 128))
    sk = np.asarray(attn_sink, f32)[0].reshape(2, 1, 4, 1)
    sinkx = np.ascontiguousarray(np.broadcast_to(sk, (2, 64, 4, 128)).reshape(128, 512))

    shared = {
        "win": win, "bg": bg, "cw": cw, "cbias": cbias, "wa": wa, "wc": wc, "wo": wo,
        "ln_in_g": np.asarray(ln_in_g, f32).reshape(1, D), "ln_in_b": np.asarray(ln_in_b, f32).reshape(1, D),
        "ln_mix_g": np.asarray(ln_mix_g, f32).reshape(1, D), "ln_mix_b": np.asarray(ln_mix_b, f32).reshape(1, D),
        "ln_ffn_g": np.asarray(ln_ffn_g, f32).reshape(1, D), "ln_ffn_b": np.asarray(ln_ffn_b, f32).reshape(1, D),
        "wup": wup, "fcw": fcw, "fcb": fcb, "wd": wd, "brel": brel, "sinkx": sinkx,
    }
    in_maps = []
    for c in range(NCORES):
        b, half = c // 2, c % 2
        s = half * TOK_CORE
        xe = np.zeros((XROWS, D), f32)
        lo, hi = s - 256, s + TOK_CORE + 256
        slo, shi = max(lo, 0), min(hi, SEQ)
        xe[slo - lo:shi - lo] = x[b, slo:shi]
        tok = lo + np.arange(XROWS)
        valid = ((tok >= 0) & (tok < SEQ)).astype(f32).reshape(NBLK_CORE, 128).T
        m = dict(shared)
        m["x"] = xe
        m["tokvalid"] = np.ascontiguousarray(valid)
        m["kbias"] = np.ascontiguousarray(np.where(valid > 0, f32(0.0), f32(NEG)).astype(f32))
        in_maps.append(m)

    if _PROGRAM is None:
        _PROGRAM = build_program()
    res = run_bass_kernel_spmd(_PROGRAM, in_maps, core_ids=list(range(NCORES)))
    if DEBUG:
        global _LAST
        _LAST = res.results
    out = np.empty((4, SEQ, D), f32)
    for c in range(NCORES):
        b, half = c // 2, c % 2
        out[b, half * TOK_CORE:(half + 1) * TOK_CORE] = res.results[c]["out"]
    return out
```
